# Optimizing a Trainium2 kernel written in Bass

```python
import jax, jax.numpy as jnp
from jax import lax
import numpy as np

D_MODEL = 2048
BATCH = 8
SEQ = 4096
DEPTH = 1
DEC_BATCH = 16
DEC_SEQ = 2048
PAST_LEN = 128

MIX_WIDTH = D_MODEL
F_WIDTH = MIX_WIDTH // 2
F_GROUPS = 8
F_CH = F_WIDTH // F_GROUPS
A_WIDTH = MIX_WIDTH - F_WIDTH
HEAD_DIM = 64
N_Q_HEADS = A_WIDTH // HEAD_DIM
N_KV_HEADS = 4
GQA_GROUP = N_Q_HEADS // N_KV_HEADS
KV_WIDTH = N_KV_HEADS * HEAD_DIM
IN_WIDTH = F_WIDTH + A_WIDTH + 2 * KV_WIDTH
WINDOW = 128
BLOCK = 128
ROPE_THETA = 10000.0
D_FF = 4 * D_MODEL
EPS = 1e-6
NEG_INF = -1e30

kernel_name = "hymba_fnet_swa_encoder"


def rmsnorm(x, g):
    xf = x.astype(jnp.float32)
    y = xf * lax.rsqrt(jnp.mean(xf * xf, axis=-1, keepdims=True) + EPS)
    return (y * g.astype(jnp.float32)).astype(x.dtype)


def rope_tables(seq_len):
    inv_freq = 1.0 / (ROPE_THETA ** (jnp.arange(0, HEAD_DIM, 2, dtype=jnp.float32) / HEAD_DIM))
    pos = jnp.arange(seq_len, dtype=jnp.float32)
    ang = pos[:, None] * inv_freq[None, :]
    ang = jnp.concatenate([ang, ang], axis=-1)
    return jnp.cos(ang), jnp.sin(ang)


def apply_rope(x, cos, sin):
    half = HEAD_DIM // 2
    x1, x2 = x[..., :half], x[..., half:]
    rot = jnp.concatenate([-x2, x1], axis=-1)
    c = cos[None, :, None, :].astype(x.dtype)
    s = sin[None, :, None, :].astype(x.dtype)
    return x * c + rot * s


def fourier_mix(zf, w_f):
    spec = jnp.fft.fft2(zf.astype(jnp.float32), axes=(1, 3), norm='ortho')
    re = spec.real.astype(zf.dtype)
    return jnp.einsum('bsgc,gcd->bsgd', re, w_f)


def window_attention(q, k, v, sink):
    B, S = q.shape[0], q.shape[1]
    nb = S // BLOCK
    qb = q.reshape(B, nb, BLOCK, N_KV_HEADS, GQA_GROUP, HEAD_DIM)
    pad = ((0, 0), (BLOCK, BLOCK), (0, 0), (0, 0))
    kp = jnp.pad(k, pad).reshape(B, nb + 2, BLOCK, N_KV_HEADS, HEAD_DIM)
    vp = jnp.pad(v, pad).reshape(B, nb + 2, BLOCK, N_KV_HEADS, HEAD_DIM)
    kb = jnp.concatenate([kp[:, :-2], kp[:, 1:-1], kp[:, 2:]], axis=2)
    vb = jnp.concatenate([vp[:, :-2], vp[:, 1:-1], vp[:, 2:]], axis=2)
    scale = HEAD_DIM ** -0.5
    s = jnp.einsum('bnqhgd,bnkhd->bnhgqk', qb, kb).astype(jnp.float32) * scale
    blk = jnp.arange(nb)[:, None] * BLOCK
    qpos = blk + jnp.arange(BLOCK)[None, :]
    kpos = blk - BLOCK + jnp.arange(3 * BLOCK)[None, :]
    valid = (jnp.abs(qpos[:, :, None] - kpos[:, None, :]) <= WINDOW) \
        & (kpos >= 0)[:, None, :] & (kpos < S)[:, None, :]
    s = jnp.where(valid[None, :, None, None, :, :], s, NEG_INF)
    sink_l = sink.astype(jnp.float32).reshape(N_KV_HEADS, GQA_GROUP)[None, None, :, :, None, None]
    m = jnp.maximum(jnp.max(s, axis=-1, keepdims=True), sink_l)
    p = jnp.exp(s - m)
    p = p / (jnp.sum(p, axis=-1, keepdims=True) + jnp.exp(sink_l - m))
    o = jnp.einsum('bnhgqk,bnkhd->bnqhgd', p.astype(v.dtype), vb)
    return o.reshape(B, S, N_Q_HEADS, HEAD_DIM)


def hybrid_layer(x, ln_mix_g, w_in, w_fourier, attn_sink, out_norm_fourier_g,
                 out_norm_attn_g, w_out, ln_mlp_g, w_up, w_down):
    B, S, _ = x.shape
    h = rmsnorm(x, ln_mix_g)
    z = h @ w_in
    o1 = F_WIDTH
    o2 = o1 + A_WIDTH
    o3 = o2 + KV_WIDTH
    zf = z[..., :o1].reshape(B, S, F_GROUPS, F_CH)
    q = z[..., o1:o2].reshape(B, S, N_Q_HEADS, HEAD_DIM)
    k = z[..., o2:o3].reshape(B, S, N_KV_HEADS, HEAD_DIM)
    v = z[..., o3:].reshape(B, S, N_KV_HEADS, HEAD_DIM)
    of = fourier_mix(zf, w_fourier).reshape(B, S, F_WIDTH)
    cos, sin = rope_tables(S)
    q = apply_rope(q, cos, sin)
    k = apply_rope(k, cos, sin)
    oa = window_attention(q, k, v, attn_sink).reshape(B, S, A_WIDTH)
    mixed = jnp.concatenate([rmsnorm(of, out_norm_fourier_g), rmsnorm(oa, out_norm_attn_g)], axis=-1)
    x = x + mixed @ w_out
    h = rmsnorm(x, ln_mlp_g)
    u = jax.nn.relu(h @ w_up)
    x = x + (u * u) @ w_down
    return x


def setup_inputs(seed: int = 0) -> dict:
    key = jax.random.key(seed)
    ks = jax.random.split(key, 16)
    f32 = jnp.float32
    nrm = lambda k, shape, s: jax.random.normal(k, shape, f32) * s
    return {
        'x_prompt': nrm(ks[0], (BATCH, SEQ, D_MODEL), 1.0),
        'x_sample': nrm(ks[1], (DEC_BATCH, DEC_SEQ, D_MODEL), 1.0),
        'ln_mix_g': 1.0 + nrm(ks[2], (DEPTH, D_MODEL), 0.02),
        'w_in': nrm(ks[3], (DEPTH, D_MODEL, IN_WIDTH), D_MODEL ** -0.5),
        'w_fourier': nrm(ks[4], (DEPTH, F_GROUPS, F_CH, F_CH), F_CH ** -0.5),
        'attn_sink': nrm(ks[5], (DEPTH, N_Q_HEADS), 0.5),
        'out_norm_fourier_g': 1.0 + nrm(ks[6], (DEPTH, F_WIDTH), 0.02),
        'out_norm_attn_g': 1.0 + nrm(ks[7], (DEPTH, A_WIDTH), 0.02),
        'w_out': nrm(ks[8], (DEPTH, MIX_WIDTH, D_MODEL), MIX_WIDTH ** -0.5),
        'ln_mlp_g': 1.0 + nrm(ks[9], (DEPTH, D_MODEL), 0.02),
        'w_up': nrm(ks[10], (DEPTH, D_MODEL, D_FF), D_MODEL ** -0.5),
        'w_down': nrm(ks[11], (DEPTH, D_FF, D_MODEL), D_FF ** -0.5),
        'ln_final_g': 1.0 + nrm(ks[12], (D_MODEL,), 0.02),
    }


def trunk(x, ln_mix_g, w_in, w_fourier, attn_sink, out_norm_fourier_g,
          out_norm_attn_g, w_out, ln_mlp_g, w_up, w_down, ln_final_g):
    for l in range(DEPTH):
        x = hybrid_layer(x, ln_mix_g[l], w_in[l], w_fourier[l], attn_sink[l],
                         out_norm_fourier_g[l], out_norm_attn_g[l], w_out[l],
                         ln_mlp_g[l], w_up[l], w_down[l])
    return rmsnorm(x, ln_final_g)


def reference(x_prompt, x_sample, ln_mix_g, w_in, w_fourier, attn_sink,
              out_norm_fourier_g, out_norm_attn_g, w_out, ln_mlp_g, w_up,
              w_down, ln_final_g):
    y_prompt = trunk(x_prompt, ln_mix_g, w_in, w_fourier, attn_sink, out_norm_fourier_g,
                     out_norm_attn_g, w_out, ln_mlp_g, w_up, w_down, ln_final_g)
    y_sample = trunk(x_sample, ln_mix_g, w_in, w_fourier, attn_sink, out_norm_fourier_g,
                     out_norm_attn_g, w_out, ln_mlp_g, w_up, w_down, ln_final_g)
    return (y_prompt, y_sample)
```

```python
from contextlib import ExitStack

import numpy as np
import ml_dtypes
import concourse.bass as bass
import concourse.mybir as mybir
from concourse.bass_utils import run_bass_kernel_spmd

F32 = mybir.dt.float32
BF16 = mybir.dt.bfloat16
AF = mybir.ActivationFunctionType
ALU = mybir.AluOpType
bf = ml_dtypes.bfloat16

D = 2048
DFF = 8192
NH = 16
NKV = 4
HD = 64
EPS = 1e-6
N_CORES = 8

ENGS = ("pe", "act", "dve", "pool", "sp")
EPOCH = 30000


class Buf:
    __slots__ = ("name", "w", "r")

    def __init__(self, name):
        self.name = name
        self.w = None
        self.r = {}


class Ins:
    __slots__ = ("eng", "fn", "deps", "dwaits", "is_dma", "dkey", "marked",
                 "sem", "val", "phase", "n")

    def __init__(self, eng, fn, is_dma, phase, n):
        self.eng = eng
        self.fn = fn
        self.deps = []
        self.dwaits = {}
        self.is_dma = is_dma
        self.dkey = None
        self.marked = False
        self.sem = None
        self.val = None
        self.phase = phase
        self.n = n


class Prog:
    def __init__(self, nc):
        self.nc = nc
        self.phase = 0
        self.lists = {e: [] for e in ENGS}
        self.dma_sem = {}
        self.dma_cnt = {}
        self.prog_sems = {e: [] for e in ENGS}
        self.nmarks = {e: 0 for e in ENGS}
        self.waited = {e: {} for e in ENGS}
        self.n = 0
        self.sem_names = 0
        self.stats = {e: 0 for e in ENGS}

    def _newsem(self, name):
        self.sem_names += 1
        return self.nc.alloc_semaphore(name="s%d_%s" % (self.sem_names, name))

    def _add(self, ins, reads, writes):
        deps = ins.deps
        for b in reads:
            if b.w is not None:
                deps.append(b.w)
        for b in writes:
            for r in b.r.values():
                deps.append(r)
            if b.w is not None:
                deps.append(b.w)
        ph = self.phase
        nd = []
        for d in deps:
            if d.phase != ph or d is ins:
                continue
            if d.is_dma:
                k = d.dkey
                ins.dwaits[k] = self.dma_cnt[k]
            elif d.eng == ins.eng and not ins.is_dma and d not in \
                    [b.w for b in reads]:
                continue
            else:
                nd.append(d)
        ins.deps = nd
        for b in reads:
            if ins.is_dma:
                b.r[("dma", ins.n)] = ins
            else:
                b.r[ins.eng] = ins
        for b in writes:
            b.w = ins
            b.r = {}
        self.lists[ins.eng].append(ins)
        self.stats[ins.eng] += 1

    def op(self, eng, fn, reads=(), writes=()):
        self.n += 1
        ins = Ins(eng, fn, False, self.phase, self.n)
        self._add(ins, reads, writes)
        return ins

    def dma(self, out, in_, key, reads=(), writes=(), q="sp", **kw):
        self.n += 1
        ins = Ins(q, (lambda e: e.dma_start(out=out, in_=in_, **kw)), True,
                  self.phase, self.n)
        ins.dkey = key
        if key not in self.dma_sem:
            self.dma_sem[key] = self._newsem(key)
            self.dma_cnt[key] = 0
        self._add(ins, reads, writes)
        self.dma_cnt[key] += 16
        return ins

    def _prog_sem(self, eng, epoch):
        lst = self.prog_sems[eng]
        while len(lst) <= epoch:
            lst.append(self._newsem("p_%s%d" % (eng, len(lst))))
        return lst[epoch]

    def end_phase(self, skip_prefix=None):
        nc = self.nc
        lists = self.lists
        for e in ENGS:
            for ins in lists[e]:
                for d in ins.deps:
                    if d.eng == "pe" and ins.eng == "pe" and not ins.is_dma:
                        continue
                    d.marked = True
        last_compute = {}
        for e in ENGS:
            for ins in reversed(lists[e]):
                if not ins.is_dma:
                    ins.marked = True
                    last_compute[e] = ins
                    break
        for e in ENGS:
            for ins in lists[e]:
                if ins.marked and not ins.is_dma:
                    c = self.nmarks[e]
                    self.nmarks[e] = c + 1
                    ins.sem = self._prog_sem(e, c // EPOCH)
                    ins.val = (c % EPOCH) + 1
        dma_final = {k: v for k, v in self.dma_cnt.items()
                     if not (skip_prefix and k.startswith(skip_prefix))}

        def emit(eng_name, eng):
            waited = self.waited[eng_name]

            def wait(sem, val):
                key = id(sem)
                if waited.get(key, 0) >= val:
                    return
                eng.wait_ge(sem, val)
                waited[key] = val

            for ins in lists[eng_name]:
                for d in ins.deps:
                    if d.eng == "pe" and eng_name == "pe" and not ins.is_dma:
                        continue
                    wait(d.sem, d.val)
                for k, v in ins.dwaits.items():
                    wait(self.dma_sem[k], v)
                bi = ins.fn(eng)
                if ins.is_dma:
                    bi.then_inc(self.dma_sem[ins.dkey], 16)
                elif ins.marked:
                    bi.then_inc(ins.sem, 1)
            for o, li in last_compute.items():
                if o != eng_name:
                    wait(li.sem, li.val)
            for k, v in dma_final.items():
                if v > 0:
                    wait(self.dma_sem[k], v)

        with nc.Block() as block:
            @block.tensor
            def _(eng):
                emit("pe", eng)

            @block.scalar
            def _(eng):
                emit("act", eng)

            @block.vector
            def _(eng):
                emit("dve", eng)

            @block.gpsimd
            def _(eng):
                emit("pool", eng)

            @block.sync
            def _(eng):
                emit("sp", eng)

        self.lists = {e: [] for e in ENGS}
        self.phase += 1


class Ring:
    def __init__(self, items):
        self.items = items
        self.i = 0

    def next(self):
        it = self.items[self.i % len(self.items)]
        self.i += 1
        return it


_CONST_CACHE = {}


def _dft_table(S):
    n = S // 128
    s = np.arange(S, dtype=np.int64)
    k = np.arange(S // 2, dtype=np.int64)
    prod = (s[:, None] * k[None, :]) % S
    ang = prod.astype(np.float64) * (2.0 * np.pi / S)
    c = np.cos(ang)
    ns = -np.sin(ang)

    def lay(m):
        m = np.concatenate([m[0::2], m[1::2]], axis=0)
        m = m.reshape(n, 128, n // 2, 128)
        return np.ascontiguousarray(m.transpose(2, 1, 0, 3)).astype(bf).reshape(n // 2, 128, n * 128)
    return lay(c), lay(ns)


def _consts(S_set, smax):
    key = (tuple(sorted(S_set)), smax)
    if key in _CONST_CACHE:
        return _CONST_CACHE[key]
    c = {}
    c["ident"] = np.eye(128, dtype=np.float32).astype(bf)
    rot = np.zeros((128, 128), np.float32)
    for d in range(128):
        dl = d % 64
        base = d - dl
        if dl < 32:
            rot[base + dl + 32, d] = -1.0
        else:
            rot[base + dl - 32, d] = 1.0
    c["rotm"] = rot.astype(bf)
    i = np.arange(128, dtype=np.int64)
    a = (i[:, None] * i[None, :]) % 128
    ang = a.astype(np.float64) * (2 * np.pi / 128)
    c["dft128"] = np.concatenate([np.cos(ang), np.sin(ang)], axis=1).astype(bf)
    inv_freq = (1.0 / (np.float32(10000.0) ** (np.arange(0, HD, 2, dtype=np.float32) / np.float32(HD)))).astype(np.float32)
    pos = np.arange(smax, dtype=np.float32)
    angr = (pos[:, None] * inv_freq[None, :]).astype(np.float32)
    angr = np.concatenate([angr, angr], axis=1)
    cosT = np.cos(angr).astype(np.float32).T
    sinT = np.sin(angr).astype(np.float32).T
    c["ropec"] = np.ascontiguousarray(np.concatenate([cosT, cosT], axis=0))
    c["ropes"] = np.ascontiguousarray(np.concatenate([sinT, sinT], axis=0))
    z = np.zeros_like(cosT)
    c["ropec_lo"] = np.ascontiguousarray(np.concatenate([cosT, z], axis=0))
    c["ropes_lo"] = np.ascontiguousarray(np.concatenate([sinT, z], axis=0))
    c["ropec_hi"] = np.ascontiguousarray(np.concatenate([z, cosT], axis=0))
    c["ropes_hi"] = np.ascontiguousarray(np.concatenate([z, sinT], axis=0))
    j = np.arange(128)[:, None]
    q = np.arange(128)[None, :]
    mp = (q <= j).astype(np.float32)
    mn = (j <= q).astype(np.float32)
    c["maskp"] = np.tile(mp, (1, 4)).astype(bf)
    c["maskn"] = np.tile(mn, (1, 4)).astype(bf)
    for S in S_set:
        tc, ts = _dft_table(S)
        c["tabc%d" % S] = tc
        c["tabs%d" % S] = ts
    _CONST_CACHE[key] = c
    return c


PW_IN, PW_OUT, PW_UP, PW_DOWN, NPIECE = 0, 6, 10, 26, 42


import os
PARTS = set(os.environ.get('P1PARTS', 'four,qk,v,att').split(','))
ATTL = int(os.environ.get('ATTL', '3'))
NOMASK = int(os.environ.get('NOMASK', '0'))
NOEXP = int(os.environ.get('NOEXP', '0'))


def build(S_list, debug=False, upto=3):
    nc = bass.Bass("TRN2", target_bir_lowering=False)
    NTOK = sum(S_list)
    S_set = sorted(set(S_list))
    SMAX = max(S_list)

    def din(name, shape, dt=F32):
        return nc.dram_tensor(name, list(shape), dt, kind="ExternalInput").ap()

    x = din("x", [NTOK, D])
    w_in = din("w_in", [D, 2560])
    w_f = din("w_fourier", [8, 128, 128])
    sink = din("attn_sink", [NH])
    g_mix = din("ln_mix_g", [D])
    g_f = din("out_norm_fourier_g", [1024])
    g_a = din("out_norm_attn_g", [1024])
    w_out = din("w_out", [D, D])
    g_mlp = din("ln_mlp_g", [D])
    w_up = din("w_up", [D, DFF])
    w_down = din("w_down", [DFF, D])
    g_fin = din("ln_final_g", [D])
    c_ident = din("ident", [128, 128], BF16)
    c_rotm = din("rotm", [128, 128], BF16)
    c_dft = din("dft128", [128, 256], BF16)
    c_ropec = din("ropec", [128, SMAX])
    c_ropes = din("ropes", [128, SMAX])
    c_rope_lh = [din(n, [128, SMAX]) for n in ("ropec_lo", "ropes_lo", "ropec_hi", "ropes_hi")]
    c_maskp = din("maskp", [128, 512], BF16)
    c_maskn = din("maskn", [128, 512], BF16)
    tabs = {}
    for S in S_set:
        n = S // 128
        tabs[S] = (din("tabc%d" % S, [n // 2, 128, n * 128], BF16),
                   din("tabs%d" % S, [n // 2, 128, n * 128], BF16))
    y = nc.dram_tensor("y", [NTOK, D], F32, kind="ExternalOutput").ap()
    skind = "ExternalOutput" if debug else "Internal"
    wp = nc.dram_tensor("wp_scr", [NPIECE, 128, 8192], BF16, kind="Internal").ap()
    xab = nc.dram_tensor("xab_scr", [NTOK, 2048], BF16, kind=skind).ap()
    mixed = nc.dram_tensor("mixed_scr", [NTOK, 2048], BF16, kind=skind).ap()

    P = Prog(nc)

    with ExitStack() as ges:
        def gsb(name, shape, dt):
            return ges.enter_context(nc.sbuf_tensor(name, list(shape), dt))

        ident = gsb("ident_sb", [128, 128], BF16)
        rotm = gsb("rotm_sb", [128, 128], BF16)
        maskp = gsb("maskp_sb", [128, 512], BF16)
        maskn = gsb("maskn_sb", [128, 512], BF16)
        AB = gsb("AB_sb", [128, 8, 256], BF16)
        esink = gsb("esink_sb", [128, NH], F32)
        epsb = gsb("eps_sb", [128, 1], F32)
        b_const = Buf("const")
        b_AB = Buf("AB")
        b_esink = Buf("esink")

        with ExitStack() as es:
            def sb(name, shape, dt):
                return es.enter_context(nc.sbuf_tensor(name, list(shape), dt))

            def ps(name):
                return es.enter_context(nc.psum_tensor(name, [128, 512], F32))

            banks = [ps("p0bank%d" % i) for i in range(2)]
            b_banks = [Buf("p0bank%d" % i) for i in range(2)]
            dft = sb("dft_sb", [128, 256], BF16)
            b_dft = Buf("dft")
            gcol = sb("gcol", [128, 3, 16], F32)
            b_gcol = Buf("gcol")
            wf_st = sb("wf_st", [128, 8, 128], F32)
            wf_b = sb("wf_b", [128, 8, 128], BF16)
            b_wfst = Buf("wfst")
            b_wfb = Buf("wfb")
            stage = [sb("stage%d" % i, [128, 16, 512], F32) for i in range(2)]
            b_stage = [Buf("stage%d" % i) for i in range(2)]
            outb = [sb("outb%d" % i, [128, 16, 512], BF16) for i in range(2)]
            b_outb = [Buf("outb%d" % i) for i in range(2)]

            P.op("pool", lambda e: e.memset(epsb[:], EPS), writes=[b_const])
            P.dma(ident[:], c_ident, "c0", writes=[b_const])
            P.dma(rotm[:], c_rotm, "c0", writes=[b_const])
            P.dma(maskp[:], c_maskp, "c0", writes=[b_const])
            P.dma(maskn[:], c_maskn, "c0", writes=[b_const])
            P.dma(dft[:], c_dft, "c1", writes=[b_dft])
            P.dma(esink[:], sink.partition_broadcast(128), "c2", writes=[b_esink])
            P.op("act", lambda e: e.activation(out=esink[:], in_=esink[:], func=AF.Exp),
                 reads=[b_esink], writes=[b_esink])
            P.dma(gcol[:, 0, :], g_mix.rearrange("(c p) -> p c", p=128), "c3",
                  writes=[b_gcol], allow_slow_non_contiguous=True)
            P.dma(gcol[:, 1, 0:8], g_f.rearrange("(c p) -> p c", p=128), "c3",
                  writes=[b_gcol], allow_slow_non_contiguous=True)
            P.dma(gcol[:, 1, 8:16], g_a.rearrange("(c p) -> p c", p=128), "c3",
                  writes=[b_gcol], allow_slow_non_contiguous=True)
            P.dma(gcol[:, 2, :], g_mlp.rearrange("(c p) -> p c", p=128), "c3",
                  writes=[b_gcol], allow_slow_non_contiguous=True)
            P.dma(wf_st[:], w_f.rearrange("g d e -> d g e"), "c4", writes=[b_wfst])
            P.op("dve", lambda e: e.tensor_copy(wf_b[:], wf_st[:]), reads=[b_wfst], writes=[b_wfb])
            for g in range(8):
                bk = banks[g % 2]
                bb = b_banks[g % 2]
                P.op("pe", lambda e, bk=bk, g=g: e.matmul(bk[:, 0:128], lhsT=dft[:, 0:128], rhs=wf_b[:, g, :],
                                                          start=True, stop=True),
                     reads=[b_dft, b_wfb], writes=[bb])
                P.op("pe", lambda e, bk=bk, g=g: e.matmul(bk[:, 128:256], lhsT=dft[:, 128:256], rhs=wf_b[:, g, :],
                                                          start=True, stop=True),
                     reads=[b_dft, b_wfb], writes=[bb])
                P.op("act", lambda e, bk=bk, g=g: e.activation(out=AB[:, g, :], in_=bk[:, 0:256], func=AF.Copy),
                     reads=[bb], writes=[b_AB])

            pieces = []

            def rows(wap, r0):
                return wap[r0:r0 + 2048, :]

            for i in range(4):
                pieces.append(([(0, w_in[:, i * 512:(i + 1) * 512])], 0, 512))
            segs = []
            for kvh in range(4):
                src = w_in[:, 2048 + kvh * 64:2048 + (kvh + 1) * 64]
                segs.append((kvh * 128, src))
                segs.append((kvh * 128 + 64, src))
            pieces.append((segs, 0, 512))
            pieces.append(([(0, w_in[:, 2304:2560])], 0, 256))
            for c in range(4):
                pieces.append(([(0, w_out[:, c * 512:(c + 1) * 512])], 1, 512))
            for i in range(16):
                pieces.append(([(0, w_up[:, i * 512:(i + 1) * 512])], None, 512))
            for c in range(4):
                for fg in range(4):
                    pieces.append(([(0, w_down[fg * 2048:(fg + 1) * 2048, c * 512:(c + 1) * 512])], None, 512))
            assert len(pieces) == NPIECE
            bg_pieces = pieces[PW_UP:]
            for pi, (segs, gsel, width) in enumerate(pieces[:PW_UP]):
                st = stage[pi % 2]
                bs = b_stage[pi % 2]
                ob = outb[pi % 2]
                bo = b_outb[pi % 2]
                for (d0, src) in segs:
                    wseg = src.shape[1]
                    P.dma(st[:, :, d0:d0 + wseg], src.rearrange("(rc p) c -> p rc c", p=128),
                          "ldst%d" % (pi % 2), writes=[bs])
                eng = ("dve", "pool")[pi % 2]
                if gsel is None:
                    P.op(eng, lambda e, st=st, ob=ob, width=width: e.tensor_copy(ob[:, :, 0:width], st[:, :, 0:width]),
                         reads=[bs], writes=[bo])
                else:
                    P.op(eng, lambda e, st=st, ob=ob, width=width, gsel=gsel: e.tensor_tensor(
                        ob[:, :, 0:width], st[:, :, 0:width],
                        gcol[:, gsel, :].unsqueeze(2).to_broadcast([128, 16, width]), ALU.mult),
                         reads=[bs, b_gcol], writes=[bo])
                P.dma(wp[pi].rearrange("p (rc c) -> p rc c", rc=16)[:, :, 0:width], ob[:, :, 0:width],
                      "stw%d" % (pi % 2), reads=[bo], q="pool")
            P.end_phase()
            if upto == 0:
                return nc, P

        class WStream:
            def __init__(self, slots, bufs, seq, tag):
                self.slots = slots
                self.bufs = bufs
                self.seq = seq
                self.issued = 0
                self.i = 0
                self.tag = tag

            def _issue(self, k):
                n = len(self.slots)
                pidx, width = self.seq[k]
                sl = self.slots[k % n]
                P.dma(sl[:, :, 0:width], wp[pidx].rearrange("p (rc c) -> p rc c", rc=16)[:, :, 0:width],
                      "%s%d" % (self.tag, k % n), writes=[self.bufs[k % n]])

            def get(self):
                n = len(self.slots)
                k = self.i
                while self.issued < min(len(self.seq), k + n):
                    self._issue(self.issued)
                    self.issued += 1
                self.i += 1
                return self.slots[k % n], self.bufs[k % n]

        seq_base = []
        b = 0
        for S in S_list:
            seq_base.append(b)
            b += S

        with ExitStack() as es:
            def sb(name, shape, dt):
                return es.enter_context(nc.sbuf_tensor(name, list(shape), dt))

            pbanks = [es.enter_context(nc.psum_tensor("p1bank%d" % i, [128, 512], F32)) for i in range(8)]
            pb = [Buf("p1bank%d" % i) for i in range(8)]
            TR = Ring([(pbanks[i][:].bitcast(BF16), pb[i]) for i in (0, 1)])
            G = Ring([(pbanks[i], pb[i]) for i in (2, 3, 4)])
            OB = [pbanks[5], pbanks[6], pbanks[7]]
            b_OB = [pb[5], pb[6], pb[7]]

            wslots = [sb("w1s%d" % i, [128, 16, 512], BF16) for i in range(3)]
            wbufs = [Buf("w1s%d" % i) for i in range(3)]
            xsr = Ring([(sb("xs%d" % i, [128, D], F32), Buf("xs%d" % i), i) for i in range(2)])
            junk = sb("junk", [128, D], BF16)
            b_junk = Buf("junk")
            hbr = Ring([(sb("hb%d" % i, [128, D], BF16), Buf("hb%d" % i)) for i in range(4)])
            hT = sb("hT", [128, 16, 512], BF16)
            b_hT = Buf("hT")
            zfT = sb("zfT", [128, 8, 512], BF16)
            b_zf = [Buf("zf%d" % g) for g in range(8)]
            xabr = Ring([(sb("xabsb%d" % i, [128, 2048], BF16), Buf("xabsb%d" % i), i) for i in range(2)])
            qT = [sb("qT%d" % i, [128, 8, 512], BF16) for i in range(2)]
            b_qT = [Buf("qT%d" % i) for i in range(2)]
            kT = [sb("kT%d" % i, [128, 2, 4, 512], BF16) for i in range(2)]
            b_kT = [Buf("kT%d" % i) for i in range(2)]
            v1 = [sb("v1_%d" % i, [128, 4, 4 * 66], BF16) for i in range(2)]
            b_v1 = [Buf("v1_%d" % i) for i in range(2)]
            ropr = Ring([(sb("rc%d" % i, [128, 512], F32), sb("rs%d" % i, [128, 512], F32),
                          Buf("rc%d" % i), Buf("rs%d" % i), i) for i in range(1)])
            roplh = Ring([([sb("rlh%d_%d" % (i, t), [128, 512], F32) for t in range(4)], Buf("rlh%d" % i), i)
                          for i in range(1)])
            qbr = Ring([(sb("qb%d" % i, [128, 512], BF16), Buf("qb%d" % i)) for i in range(2)])
            t1r = Ring([(sb("t1_%d" % i, [128, 512], F32), Buf("t1_%d" % i)) for i in range(2)])
            t2r = Ring([(sb("t2_%d" % i, [128, 512], F32), Buf("t2_%d" % i)) for i in range(2)])
            pTr = Ring([(sb("pT%d" % i, [128, 512], BF16), Buf("pT%d" % i)) for i in range(6)])
            den = sb("den", [128, NH], F32)
            b_den = [Buf("den%d" % i) for i in range(3)]
            oar = Ring([(sb("oa%d" % i, [128, 1024], F32), Buf("oa%d" % i)) for i in range(2)])
            oanr = Ring([(sb("oan%d" % i, [128, 1024], BF16), Buf("oan%d" % i), i) for i in range(2)])
            ssr = Ring([(sb("ss%d" % i, [128, 2], F32), Buf("ss%d" % i)) for i in range(4)])

            def norm_to_bf16(src, b_src, dst, b_dst, width):
                ss, b_ss = ssr.next()
                P.op("act", lambda e: e.activation(out=junk[:, 0:width], in_=src, func=AF.Square,
                                                   accum_out=ss[:, 0:1]),
                     reads=[b_src], writes=[b_junk, b_ss])
                P.op("act", lambda e: e.activation(out=ss[:, 1:2], in_=ss[:, 0:1], func=AF.Ln, scale=1.0 / width,
                                                   bias=epsb[:, 0:1]),
                     reads=[b_ss, b_const], writes=[b_ss])
                P.op("act", lambda e: e.activation(out=ss[:, 1:2], in_=ss[:, 1:2], func=AF.Exp, scale=-0.5),
                     reads=[b_ss], writes=[b_ss])
                P.op("dve", lambda e: e.tensor_tensor(dst, src, ss[:, 1:2].to_broadcast([128, width]), ALU.mult),
                     reads=[b_src, b_ss], writes=[b_dst])

            for i in range(2):
                P.op("pool", lambda e, i=i: e.memset(v1[i][:], 1.0), writes=[b_v1[i]])

            seq1 = []
            for S in S_list:
                for T in range(S // 512):
                    seq1 += [(PW_IN + i, 512) for i in range(5)] + [(PW_IN + 5, 256)]
            ws = WStream(wslots, wbufs, seq1, "w1_")
            evq = [0]

            def evac_eng():
                evq[0] += 1
                return ("dve", "act")[evq[0] % 2]

            def copy_op(eng, out, in_, reads, writes):
                if eng == "act":
                    P.op("act", lambda e: e.activation(out=out, in_=in_, func=AF.Copy), reads=reads, writes=writes)
                else:
                    P.op(eng, lambda e: e.tensor_copy(out, in_), reads=reads, writes=writes)

            def sc_tiles(bq, kvh, NB):
                qs = (bq // 4) % 2
                qo = (bq % 4) * 128
                pts = []
                for j in (bq - 1, bq, bq + 1):
                    if j < 0 or j >= NB:
                        continue
                    ks = (j // 4) % 2
                    ko = (j % 4) * 128
                    bank, bbank = G.next()
                    for g in range(4):
                        pair = 2 * kvh + g // 2
                        hf = g % 2
                        P.op("pe", lambda e, bank=bank, g=g, ks=ks, ko=ko, hf=hf, pair=pair, kvh=kvh, qs=qs, qo=qo: e.matmul(
                            bank[:, g * 128:(g + 1) * 128],
                            lhsT=kT[ks][:, hf, kvh, ko:ko + 128],
                            rhs=qT[qs][:, pair, qo:qo + 128],
                            start=True, stop=True),
                             reads=[b_kT[ks], b_qT[qs]], writes=[bbank])
                    pt, b_pt = pTr.next()
                    P.op("act", lambda e, pt=pt, bank=bank: e.activation(out=pt[:], in_=bank[:], func=AF.Exp,
                                                                         scale=HD ** -0.5),
                         reads=[bbank], writes=[b_pt])
                    if j != bq:
                        mk = maskp if j < bq else maskn
                        P.op("dve", lambda e, pt=pt, mk=mk: e.tensor_tensor(pt[:], pt[:], mk[:], ALU.mult),
                             reads=[b_pt], writes=[b_pt])
                    pts.append((pt, b_pt, j))
                return pts

            def pv(kvh, pts):
                for g in range(4):
                    h = 4 * kvh + g
                    ob = OB[h // 7]
                    bob = b_OB[h // 7]
                    oc = (h % 7) * 66
                    for idx, (pt, b_pt, j) in enumerate(pts):
                        ks = (j // 4) % 2
                        jb = j % 4
                        P.op("pe", lambda e, ob=ob, oc=oc, pt=pt, g=g, ks=ks, jb=jb, idx=idx, n=len(pts), kvh=kvh: e.matmul(
                            ob[:, oc:oc + 66], lhsT=pt[:, g * 128:(g + 1) * 128],
                            rhs=v1[ks][:, jb, kvh * 66:(kvh + 1) * 66],
                            start=(idx == 0), stop=(idx == n - 1)),
                             reads=[b_pt, b_v1[ks]], writes=[bob])

            OBH = ((0, 7), (7, 7), (14, 2))

            def norm_bank(bk, oa, b_oa):
                h0, nh = OBH[bk]
                dn = den[:, h0:h0 + nh]
                P.op("dve", lambda e: e.tensor_tensor(
                    dn, OB[bk][:, 0:nh * 66].rearrange("p (h e) -> p h e", e=66)[:, :, 64],
                    esink[:, h0:h0 + nh], ALU.add),
                     reads=[b_OB[bk], b_esink], writes=[b_den[bk]])
                P.op("dve", lambda e: e.reciprocal(dn, dn), reads=[b_den[bk]], writes=[b_den[bk]])
                P.op("dve", lambda e: e.tensor_tensor(
                    oa[:, h0 * 64:(h0 + nh) * 64].rearrange("p (h e) -> p h e", e=64),
                    OB[bk][:, 0:nh * 66].rearrange("p (h e) -> p h e", e=66)[:, :, 0:64],
                    dn.unsqueeze(2).to_broadcast([128, nh, 64]), ALU.mult),
                     reads=[b_OB[bk], b_den[bk]], writes=[b_oa])

            pending = []

            def flush_pending():
                while pending:
                    pending.pop(0)[1]()

            def attention_stream(blocks, NB, tb):
                groups = [(bq, kvh) for bq in blocks for kvh in range(4)]
                if not groups:
                    return
                oas = {}
                nxt = sc_tiles(groups[0][0], groups[0][1], NB)
                for i, (bq, kvh) in enumerate(groups):
                    cur = nxt
                    if i + 1 < len(groups):
                        nxt = sc_tiles(groups[i + 1][0], groups[i + 1][1], NB)
                    if pending:
                        pending.pop(0)[1]()
                    if kvh == 0:
                        oas[bq] = oar.next()
                    oa, b_oa = oas[bq]
                    pv(kvh, cur)
                    if kvh == 1:
                        norm_bank(0, oa, b_oa)
                    if kvh == 3:
                        norm_bank(1, oa, b_oa)
                        norm_bank(2, oa, b_oa)

                        def fin(oa=oa, b_oa=b_oa, bq=bq):
                            oan, b_oan, oi = oanr.next()
                            norm_to_bf16(oa[:], b_oa, oan[:], b_oan, 1024)
                            r0 = tb + bq * 128
                            P.dma(mixed[r0:r0 + 128, 1024:2048], oan[:], "stoan%d" % oi, reads=[b_oan], q="pool")
                        pending.append(("fin", fin))

            def x_load(tok0, j):
                xs, b_xs, xi = xsr.next()
                P.dma(xs[:], x[tok0 + j * 128:tok0 + (j + 1) * 128, :], "ldxs%d" % xi, writes=[b_xs])
                return xs, b_xs

            def prep_sub(tok0, j, hbs, xq=None):
                if xq is None:
                    xs, b_xs = x_load(tok0, j)
                else:
                    xs, b_xs = xq.pop(0)
                hb, b_hb = hbr.next()
                norm_to_bf16(xs[:], b_xs, hb[:], b_hb, D)
                hbs.append((hb, b_hb))
                if xq is not None and j + 2 < 4:
                    xq.append(x_load(tok0, j + 2))

            def prep_tile(tok0):
                hbs = []
                for j in range(4):
                    prep_sub(tok0, j, hbs)
                return hbs

            def rope_start(bank, bbank, outs):
                qb, b_qb = qbr.next()
                tok = Buf("tok")
                copy_op("act", qb[:], bank[:], [bbank], [b_qb, tok])
                return (bank, bbank, outs, qb, b_qb, tok)

            def rope_finish(st):
                bank, bbank, outs, qb, b_qb, tok = st
                bank2, bbank2 = pbanks[1], pb[1]
                P.op("pe", lambda e: e.matmul(bank2[:], lhsT=rotm[:], rhs=qb[:], start=True, stop=True),
                     reads=[b_qb, b_const], writes=[bbank2])
                for (dst, b_dst, rc, rs, b_rc, b_rs) in outs:
                    rope_out(bank, bbank, bank2, bbank2, tok, dst, b_dst, rc, rs, b_rc, b_rs)

            def rope_out(bank, bbank, bank2, bbank2, tok, dst, b_dst, rc, rs, b_rc, b_rs):
                t1, b_t1 = t1r.next()
                t2, b_t2 = t2r.next()
                P.op("dve", lambda e: e.tensor_tensor(t1[:], bank[:], rc[:], ALU.mult),
                     reads=[bbank, b_rc, tok], writes=[b_t1])
                P.op("dve", lambda e: e.tensor_tensor(t2[:], bank2[:], rs[:], ALU.mult),
                     reads=[bbank2, b_rs], writes=[b_t2])
                P.op("pool", lambda e: e.tensor_tensor(dst, t1[:], t2[:], ALU.add),
                     reads=[b_t1, b_t2], writes=[b_dst])

            tiles = []
            for si, S in enumerate(S_list):
                for T in range(S // 512):
                    tiles.append((si, T))
            bg_todo = list(range(len(bg_pieces)))
            bg_per_tile = -(-len(bg_todo) // len(tiles))

            def bg_issue(n):
                for _ in range(n):
                    if not bg_todo:
                        return
                    bi = bg_todo.pop(0)
                    segs, gsel, width = bg_pieces[bi]
                    (d0, src), = segs
                    P.dma(wp[PW_UP + bi].rearrange("p (rc c) -> p rc c", rc=16),
                          src.rearrange("(rc p) c -> p rc c", p=128), "bg%d" % (bi % 4), q="pool")
            hbs_next = prep_tile(seq_base[0])
            att_next = 0
            for ti, (si, T) in enumerate(tiles):
                S = S_list[si]
                tb = seq_base[si]
                NT = S // 512
                NB = S // 128
                if T == 0:
                    att_next = 0
                tok0 = tb + T * 512
                ts_ = T % 2
                hbs = hbs_next
                rc, rs, b_rc, b_rs, ri = ropr.next()
                P.dma(rc[:], c_ropec[:, T * 512:(T + 1) * 512], "ldrc%d" % ri, writes=[b_rc])
                P.dma(rs[:], c_ropes[:, T * 512:(T + 1) * 512], "ldrs%d" % ri, writes=[b_rs])
                rlh, b_rlh, rli = roplh.next()
                for t in range(4):
                    P.dma(rlh[t][:], c_rope_lh[t][:, T * 512:(T + 1) * 512], "ldrlh%d" % rli, writes=[b_rlh])
                for j in range(4):
                    hb, b_hb = hbs[j]
                    for half in range(2):
                        trv, b_tr = TR.next()
                        for kk in range(8):
                            kc = half * 8 + kk
                            P.op("pe", lambda e, trv=trv, kk=kk, kc=kc, hb=hb: e.transpose(
                                trv[:, kk * 128:(kk + 1) * 128], hb[:, kc * 128:(kc + 1) * 128], ident[:]),
                                 reads=[b_hb, b_const], writes=[b_tr])
                        copy_op(evac_eng(), hT[:, half * 8:(half + 1) * 8, j * 128:(j + 1) * 128],
                                trv[:, 0:1024].rearrange("p (a b) -> p a b", a=8), [b_tr], [b_hT])
                for pc in range(2):
                    wsl, b_w = ws.get()
                    for gi in range(4):
                        g = pc * 4 + gi
                        bank, bbank = G.next()
                        for kc in range(16):
                            P.op("pe", lambda e, bank=bank, wsl=wsl, kc=kc, gi=gi: e.matmul(
                                bank[:], lhsT=wsl[:, kc, gi * 128:(gi + 1) * 128], rhs=hT[:, kc, :],
                                start=(kc == 0), stop=(kc == 15)),
                                 reads=[b_w, b_hT], writes=[bbank])
                        copy_op("act", zfT[:, g, :], bank[:], [bbank], [b_zf[g]])
                for j in range(4):
                    xsb, b_xsb, xbi = xabr.next()
                    for gp in range(4):
                        bank, bbank = G.next()
                        for gg in range(2):
                            g = gp * 2 + gg
                            P.op("pe", lambda e, bank=bank, gg=gg, g=g, j=j: e.matmul(
                                bank[:, gg * 256:(gg + 1) * 256], lhsT=zfT[:, g, j * 128:(j + 1) * 128],
                                rhs=AB[:, g, :], start=True, stop=True),
                                 reads=[b_zf[g], b_AB], writes=[bbank])
                        hf = gp // 2
                        q0 = (2 * gp) % 4
                        dst = xsb[:].rearrange("p (h a q e) -> p h a q e", h=2, a=2, q=4)[:, hf, :, q0:q0 + 2, :]
                        src = bank[:].rearrange("p (g a e) -> p a g e", g=2, a=2)
                        copy_op("dve", dst, src, [bbank], [b_xsb])
                    r0 = tok0 + j * 128
                    P.dma(xab[r0:r0 + 128, :], xsb[:], "stxab%d" % xbi, reads=[b_xsb], q="pool")
                bg_issue(bg_per_tile)
                rope_prev = None
                for pc in range(2, 5):
                    wsl, b_w = ws.get()
                    for ci in range(4):
                        bank, bbank = G.next()
                        for kc in range(16):
                            P.op("pe", lambda e, bank=bank, wsl=wsl, kc=kc, ci=ci: e.matmul(
                                bank[:], lhsT=wsl[:, kc, ci * 128:(ci + 1) * 128], rhs=hT[:, kc, :],
                                start=(kc == 0), stop=(kc == 15)),
                                 reads=[b_w, b_hT], writes=[bbank])
                        if pc < 4:
                            p = (pc - 2) * 4 + ci
                            st = rope_start(bank, bbank, [(qT[ts_][:, p, :], b_qT[ts_], rc, rs, b_rc, b_rs)])
                        else:
                            st = rope_start(bank, bbank,
                                            [(kT[ts_][:, 0, ci, :], b_kT[ts_], rlh[0], rlh[1], b_rlh, b_rlh),
                                             (kT[ts_][:, 1, ci, :], b_kT[ts_], rlh[2], rlh[3], b_rlh, b_rlh)])
                        if rope_prev is not None:
                            rope_finish(rope_prev)
                        rope_prev = st
                rope_finish(rope_prev)
                wsl, b_w = ws.get()
                for j in range(4):
                    bank, bbank = G.next()
                    for kc in range(16):
                        P.op("pe", lambda e, bank=bank, wsl=wsl, kc=kc, j=j: e.matmul(
                            bank[:, 0:256], lhsT=hT[:, kc, j * 128:(j + 1) * 128], rhs=wsl[:, kc, 0:256],
                            start=(kc == 0), stop=(kc == 15)),
                             reads=[b_w, b_hT], writes=[bbank])
                    copy_op("act", v1[ts_][:, j, :].rearrange("p (h e) -> p h e", e=66)[:, :, 0:64],
                            bank[:, 0:256].rearrange("p (h e) -> p h e", e=64), [bbank], [b_v1[ts_]])
                preps = []
                if ti + 1 < len(tiles):
                    nsi, nT = tiles[ti + 1]
                    hbs_next = []
                    ntok0 = seq_base[nsi] + nT * 512
                    xq = [x_load(ntok0, 0), x_load(ntok0, 1)]
                    preps = [(lambda j=j, ntok0=ntok0, hl=hbs_next, xq=xq: prep_sub(ntok0, j, hl, xq)) for j in range(4)]
                last = 4 * T + 2 if T < NT - 1 else NB - 1
                for p in preps:
                    pending.append(("prep", p))
                preps = []
                attention_stream(list(range(att_next, last + 1)), NB, tb)
                att_next = last + 1
                keep = []
                for kind, p in pending:
                    if kind == "prep":
                        p()
                    else:
                        keep.append((kind, p))
                del pending[:]
                pending.extend(keep[-1:])
                for kind, p in keep[:-1]:
                    p()
            flush_pending()
            bg_issue(len(bg_todo))
            P.end_phase(skip_prefix="bg")
            if upto == 1:
                return nc, P

        with ExitStack() as es:
            def sb(name, shape, dt):
                return es.enter_context(nc.sbuf_tensor(name, list(shape), dt))

            NSCM = SMAX // 128
            pbanks = [es.enter_context(nc.psum_tensor("p2bank%d" % i, [128, 512], F32)) for i in range(4)]
            pb2 = [Buf("p2bank%d" % i) for i in range(4)]
            PAIR = Ring([((pbanks[0], pb2[0]), (pbanks[1], pb2[1])), ((pbanks[2], pb2[2]), (pbanks[3], pb2[3]))])
            xr = sb("xabr", [128, NSCM, 1024], BF16)
            b_xr = [Buf("xabr%d" % i) for i in range(NSCM)]
            tabr = Ring([(sb("tc%d" % i, [128, NSCM * 128], BF16), sb("tn%d" % i, [128, NSCM * 128], BF16),
                          Buf("tc%d" % i), Buf("tn%d" % i), i) for i in range(3)])
            esr = Ring([(sb("es%d" % i, [128, 512], F32), Buf("es%d" % i)) for i in range(2)])
            ofr = Ring([(sb("of%d" % i, [128, 512], BF16), Buf("of%d" % i), i) for i in range(4)])
            for si, S in enumerate(S_list):
                tb = seq_base[si]
                NSC = S // 128
                NH2 = NSC // 2
                gsz = 4 if NH2 >= 4 else NH2
                tcd, tsd = tabs[S]
                scale = float(1.0 / np.sqrt(S * 128.0))
                for hf in range(2):
                    xv = xab[tb:tb + S, hf * 1024:(hf + 1) * 1024].rearrange("(m two) c -> two m c", two=2)
                    for r in range(2):
                        for grp in range(NH2 // gsz):
                            c0 = r * NH2 + grp * gsz
                            P.dma(xr[:, c0:c0 + gsz, :],
                                  xv[r][grp * gsz * 128:(grp + 1) * gsz * 128].rearrange("(sc p) c -> p sc c", p=128),
                                  "ldxr%d" % ((c0 // gsz) % 8), writes=[b_xr[c] for c in range(c0, c0 + gsz)])
                    for kt in range(NH2):
                        tcs, tns, b_tc, b_tn, ti = tabr.next()
                        P.dma(tcs[:, 0:NSC * 128], tcd[kt], "ldtc%d" % ti, writes=[b_tc])
                        P.dma(tns[:, 0:NSC * 128], tsd[kt], "ldtn%d" % ti, writes=[b_tn])
                        (accE, b_E), (accO, b_O) = PAIR.next()
                        for r, (acc, b_acc) in enumerate(((accE, b_E), (accO, b_O))):
                            for mc in range(NH2):
                                sc = r * NH2 + mc
                                P.op("pe", lambda e, acc=acc, tcs=tcs, sc=sc, mc=mc: e.matmul(
                                    acc[:], lhsT=tcs[:, sc * 128:(sc + 1) * 128], rhs=xr[:, sc, 0:512],
                                    start=(mc == 0), stop=False),
                                     reads=[b_tc, b_xr[sc]], writes=[b_acc])
                                P.op("pe", lambda e, acc=acc, tns=tns, sc=sc, mc=mc, NH2=NH2: e.matmul(
                                    acc[:], lhsT=tns[:, sc * 128:(sc + 1) * 128], rhs=xr[:, sc, 512:1024],
                                    start=False, stop=(mc == NH2 - 1)),
                                     reads=[b_tn, b_xr[sc]], writes=[b_acc])
                        esb, b_es = esr.next()
                        P.op("act", lambda e, esb=esb, accE=accE, scale=scale: e.activation(
                            out=esb[:], in_=accE[:], func=AF.Copy, scale=scale),
                             reads=[b_E], writes=[b_es])
                        for sgn, rbase in ((1.0, tb + kt * 128), (-1.0, tb + S // 2 + kt * 128)):
                            of, b_of, oi = ofr.next()
                            P.op("dve", lambda e, of=of, accO=accO, esb=esb, sc_=sgn * scale: e.scalar_tensor_tensor(
                                of[:], accO[:], sc_, esb[:], ALU.mult, ALU.add),
                                 reads=[b_O, b_es], writes=[b_of])
                            P.dma(mixed[rbase:rbase + 128, hf * 512:(hf + 1) * 512], of[:], "stof%d" % oi,
                                  reads=[b_of], q="pool")
            P.end_phase()

            if upto == 2:
                return nc, P

        with ExitStack() as es:
            def sb(name, shape, dt):
                return es.enter_context(nc.sbuf_tensor(name, list(shape), dt))

            pbanks = [es.enter_context(nc.psum_tensor("p3bank%d" % i, [128, 512], F32)) for i in range(8)]
            pb = [Buf("p3bank%d" % i) for i in range(8)]
            TR = Ring([(pbanks[i][:].bitcast(BF16), pb[i]) for i in (0, 1)])
            G = Ring([(pbanks[i], pb[i]) for i in (2, 3)])
            ACCS = [(pbanks[i], pb[i]) for i in (4, 5, 6, 7)]

            wslots = [sb("w3s%d" % i, [128, 16, 512], BF16) for i in range(3)]
            wbufs = [Buf("w3s%d" % i) for i in range(3)]
            xsr = Ring([(sb("x3_%d" % i, [128, D], F32), Buf("x3_%d" % i), i) for i in range(8)])
            mxr = Ring([(sb("mx%d" % i, [128, D], BF16), Buf("mx%d" % i), i) for i in range(4)])
            junk = sb("junk3", [128, D], BF16)
            b_junk = Buf("junk3")
            aT = sb("aT", [128, 16, 512], BF16)
            b_aT = Buf("aT")
            uT = sb("uT", [128, 16, 512], BF16)
            b_uT = [Buf("uT%d" % i) for i in range(16)]
            rlr = Ring([(sb("rl%d" % i, [128, 512], F32), Buf("rl%d" % i)) for i in range(2)])
            ssr = Ring([(sb("ss3_%d" % i, [128, 2], F32), Buf("ss3_%d" % i)) for i in range(8)])
            gfin = sb("gfin_sb", [128, D], F32)
            gmlp = sb("gmlp_sb", [128, D], F32)
            b_g3 = Buf("g3")
            P.dma(gfin[:], g_fin.partition_broadcast(128), "c5", writes=[b_g3])
            P.dma(gmlp[:], g_mlp.partition_broadcast(128), "c5", writes=[b_g3])

            def rstd_of(src, b_src, width):
                ss, b_ss = ssr.next()
                P.op("act", lambda e: e.activation(out=junk[:, 0:width], in_=src, func=AF.Square,
                                                   accum_out=ss[:, 0:1]),
                     reads=[b_src], writes=[b_junk, b_ss])
                P.op("act", lambda e: e.activation(out=ss[:, 1:2], in_=ss[:, 0:1], func=AF.Ln, scale=1.0 / width,
                                                   bias=epsb[:, 0:1]),
                     reads=[b_ss, b_const], writes=[b_ss])
                P.op("act", lambda e: e.activation(out=ss[:, 1:2], in_=ss[:, 1:2], func=AF.Exp, scale=-0.5),
                     reads=[b_ss], writes=[b_ss])
                return ss, b_ss

            NT3 = NTOK // 512
            seq3 = [(PW_OUT + c, 512) for c in range(4)]
            for T in range(NT3):
                for q in range(4):
                    seq3 += [(PW_UP + q * 4 + i, 512) for i in range(4)]
                    if q == 3 and T + 1 < NT3:
                        seq3 += [(PW_OUT + c, 512) for c in range(4)]
                    seq3 += [(PW_DOWN + c * 4 + q, 512) for c in range(4)]
            ws = WStream(wslots, wbufs, seq3, "w3_")
            evq = [0]

            def transposes(hb, b_hb, j):
                for half in range(2):
                    trv, b_tr = TR.next()
                    for kk in range(8):
                        kc = half * 8 + kk
                        P.op("pe", lambda e, trv=trv, kk=kk, kc=kc, hb=hb: e.transpose(
                            trv[:, kk * 128:(kk + 1) * 128], hb[:, kc * 128:(kc + 1) * 128], ident[:]),
                             reads=[b_hb, b_const], writes=[b_tr])
                    evq[0] += 1
                    dst = aT[:, half * 8:(half + 1) * 8, j * 128:(j + 1) * 128]
                    src = trv[:, 0:1024].rearrange("p (a b) -> p a b", a=8)
                    if evq[0] % 2:
                        P.op("dve", lambda e, dst=dst, src=src: e.tensor_copy(dst, src), reads=[b_tr], writes=[b_aT])
                    else:
                        P.op("act", lambda e, dst=dst, src=src: e.activation(out=dst, in_=src, func=AF.Copy),
                             reads=[b_tr], writes=[b_aT])

            def stage_A_loads(T):
                tok0 = T * 512
                xt, mt = [], []
                for j in range(4):
                    mx, b_mx, mi = mxr.next()
                    r0 = tok0 + j * 128
                    P.dma(mx[:], mixed[r0:r0 + 128, :], "ldmx%d" % mi, writes=[b_mx])
                    xs, b_xs, xi = xsr.next()
                    P.dma(xs[:], x[r0:r0 + 128, :], "ldx3_%d" % xi, writes=[b_xs], q=("pool" if T >= 2 else "sp"))
                    xt.append((xs, b_xs, xi))
                    mt.append((mx, b_mx))
                return xt, mt

            def stage_A_compute(mt):
                for j in range(4):
                    mx, b_mx = mt[j]
                    ss, b_ss = rstd_of(mx[:, 0:1024], b_mx, 1024)
                    P.op("act", lambda e, mx=mx, ss=ss: e.activation(out=mx[:, 0:1024], in_=mx[:, 0:1024],
                                                                     func=AF.Copy, scale=ss[:, 1:2]),
                         reads=[b_mx, b_ss], writes=[b_mx])

            def stage_A(T):
                xt, mt = stage_A_loads(T)
                stage_A_compute(mt)
                return xt, mt

            def stage_BC(xt, mt):
                for j in range(4):
                    transposes(mt[j][0], mt[j][1], j)
                for c in range(4):
                    wsl, b_w = ws.get()
                    for j in range(4):
                        xs, b_xs, xi = xt[j]
                        bank, bbank = G.next()
                        for kc in range(16):
                            P.op("pe", lambda e, bank=bank, wsl=wsl, kc=kc, j=j: e.matmul(
                                bank[:], lhsT=aT[:, kc, j * 128:(j + 1) * 128], rhs=wsl[:, kc, :],
                                start=(kc == 0), stop=(kc == 15)),
                                 reads=[b_w, b_aT], writes=[bbank])
                        P.op("dve", lambda e, bank=bank, xs=xs, c=c: e.tensor_tensor(
                            xs[:, c * 512:(c + 1) * 512], bank[:], xs[:, c * 512:(c + 1) * 512], ALU.add),
                             reads=[bbank, b_xs], writes=[b_xs])
                for j in range(4):
                    xs, b_xs, xi = xt[j]
                    mx, b_mx = mt[j]
                    ss, b_ss = rstd_of(xs[:], b_xs, D)
                    P.op("dve", lambda e, mx=mx, xs=xs, ss=ss: e.scalar_tensor_tensor(
                        mx[:], xs[:], ss[:, 1:2], gmlp[:], ALU.mult, ALU.mult),
                         reads=[b_xs, b_ss, b_g3], writes=[b_mx])

            def stage_E(mt):
                for j in range(4):
                    transposes(mt[j][0], mt[j][1], j)

            def stage_up(q):
                for i in range(4):
                    wsl, b_w = ws.get()
                    for fi in range(4):
                        f = i * 4 + fi
                        bank, bbank = G.next()
                        for kc in range(16):
                            P.op("pe", lambda e, bank=bank, wsl=wsl, kc=kc, fi=fi: e.matmul(
                                bank[:], lhsT=wsl[:, kc, fi * 128:(fi + 1) * 128], rhs=aT[:, kc, :],
                                start=(kc == 0), stop=(kc == 15)),
                                 reads=[b_w, b_aT], writes=[bbank])
                        rl, b_rl = rlr.next()
                        P.op("act", lambda e, rl=rl, bank=bank: e.activation(out=rl[:], in_=bank[:], func=AF.Relu),
                             reads=[bbank], writes=[b_rl])
                        P.op("pool", lambda e, rl=rl, f=f: e.tensor_tensor(uT[:, f, :], rl[:], rl[:], ALU.mult),
                             reads=[b_rl], writes=[b_uT[f]])

            def stage_down(xt):
                for c in range(4):
                    wsl, b_w = ws.get()
                    for jp in range(2):
                        for fl in range(16):
                            for j in (2 * jp, 2 * jp + 1):
                                acc, b_acc = ACCS[j]
                                P.op("pe", lambda e, acc=acc, fl=fl, j=j, wsl=wsl: e.matmul(
                                    acc[:], lhsT=uT[:, fl, j * 128:(j + 1) * 128], rhs=wsl[:, fl, :],
                                    start=(fl == 0), stop=(fl == 15)),
                                     reads=[b_w, b_uT[fl]], writes=[b_acc])
                        for j in (2 * jp, 2 * jp + 1):
                            xs, b_xs, xi = xt[j]
                            acc, b_acc = ACCS[j]
                            P.op("dve", lambda e, acc=acc, xs=xs, c=c: e.tensor_tensor(
                                xs[:, c * 512:(c + 1) * 512], acc[:], xs[:, c * 512:(c + 1) * 512], ALU.add),
                                 reads=[b_acc, b_xs], writes=[b_xs])

            def stage_H(T, xt):
                tok0 = T * 512
                for j in range(4):
                    xs, b_xs, xi = xt[j]
                    ss, b_ss = rstd_of(xs[:], b_xs, D)
                    P.op("dve", lambda e, xs=xs, ss=ss: e.scalar_tensor_tensor(
                        xs[:], xs[:], ss[:, 1:2], gfin[:], ALU.mult, ALU.mult),
                         reads=[b_xs, b_ss, b_g3], writes=[b_xs])
                    r0 = tok0 + j * 128
                    P.dma(y[r0:r0 + 128, :], xs[:], "sty%d" % xi, reads=[b_xs], q="pool")

            xt, mt = stage_A(0)
            stage_BC(xt, mt)
            prev_H = None
            for T in range(NT3):
                stage_E(mt)
                nxt = None
                for q in range(4):
                    stage_up(q)
                    if q == 0:
                        if prev_H is not None:
                            stage_H(*prev_H)
                        if T + 1 < NT3:
                            nxt = stage_A_loads(T + 1)
                    if q == 3 and nxt is not None:
                        stage_BC(nxt[0], nxt[1])
                    stage_down(xt)
                    if q == 0 and nxt is not None:
                        stage_A_compute(nxt[1])
                prev_H = (T, xt)
                if nxt is not None:
                    xt, mt = nxt
            stage_H(*prev_H)
            P.end_phase()

    return nc, P


_NC_CACHE = {}


def run_cores(xs_per_core, S_list, weights, debug=False, upto=3):
    key = (tuple(S_list), debug, upto)
    if key not in _NC_CACHE:
        _NC_CACHE[key] = build(S_list, debug, upto)[0]
    nc = _NC_CACHE[key]
    consts = _consts(set(S_list), max(S_list))
    in_maps = []
    for xc in xs_per_core:
        m = {"x": xc}
        m.update(weights)
        m.update(consts)
        in_maps.append(m)
    res = run_bass_kernel_spmd(nc, in_maps, core_ids=list(range(len(xs_per_core))))
    return res


def kernel(x_prompt, x_sample, ln_mix_g, w_in, w_fourier, attn_sink, out_norm_fourier_g,
           out_norm_attn_g, w_out, ln_mlp_g, w_up, w_down, ln_final_g):
    f = lambda a: np.ascontiguousarray(np.asarray(a, dtype=np.float32))
    x_prompt = f(x_prompt)
    x_sample = f(x_sample)
    BP, SP, _ = x_prompt.shape
    BS, SS, _ = x_sample.shape
    assert BP == N_CORES and BS == 2 * N_CORES
    S_list = [SP, SS, SS]
    weights = {
        "w_in": f(w_in)[0], "w_fourier": f(w_fourier)[0], "attn_sink": f(attn_sink)[0],
        "ln_mix_g": f(ln_mix_g)[0], "out_norm_fourier_g": f(out_norm_fourier_g)[0],
        "out_norm_attn_g": f(out_norm_attn_g)[0], "w_out": f(w_out)[0], "ln_mlp_g": f(ln_mlp_g)[0],
        "w_up": f(w_up)[0], "w_down": f(w_down)[0], "ln_final_g": f(ln_final_g),
    }
    xs = []
    for i in range(N_CORES):
        xs.append(np.concatenate([x_prompt[i], x_sample[2 * i], x_sample[2 * i + 1]], axis=0))
    res = run_cores(xs, S_list, weights)
    yp = np.empty((BP, SP, D), np.float32)
    ysm = np.empty((BS, SS, D), np.float32)
    for i in range(N_CORES):
        yc = res.results[i]["y"]
        yp[i] = yc[0:SP]
        ysm[2 * i] = yc[SP:SP + SS]
        ysm[2 * i + 1] = yc[SP + SS:SP + 2 * SS]
    return (yp, ysm)
```

```python
from contextlib import ExitStack

import numpy as np
import ml_dtypes
import concourse.bass as bass
import concourse.mybir as mybir
from concourse.bass_utils import run_bass_kernel_spmd

F32 = mybir.dt.float32
BF16 = mybir.dt.bfloat16
AF = mybir.ActivationFunctionType
ALU = mybir.AluOpType
bf = ml_dtypes.bfloat16

D = 2048
DFF = 8192
NH = 16
NKV = 4
HD = 64
EPS = 1e-6
N_CORES = 8

ENGS = ("pe", "act", "dve", "pool", "sp")
EPOCH = 30000


class Buf:
    __slots__ = ("name", "w", "r")

    def __init__(self, name):
        self.name = name
        self.w = None
        self.r = {}


class Ins:
    __slots__ = ("eng", "fn", "deps", "dwaits", "is_dma", "dkey", "marked",
                 "sem", "val", "phase", "n")

    def __init__(self, eng, fn, is_dma, phase, n):
        self.eng = eng
        self.fn = fn
        self.deps = []
        self.dwaits = {}
        self.is_dma = is_dma
        self.dkey = None
        self.marked = False
        self.sem = None
        self.val = None
        self.phase = phase
        self.n = n


class Prog:
    def __init__(self, nc):
        self.nc = nc
        self.phase = 0
        self.lists = {e: [] for e in ENGS}
        self.dma_sem = {}
        self.dma_cnt = {}
        self.prog_sems = {e: [] for e in ENGS}
        self.nmarks = {e: 0 for e in ENGS}
        self.waited = {e: {} for e in ENGS}
        self.n = 0
        self.sem_names = 0
        self.stats = {e: 0 for e in ENGS}

    def _newsem(self, name):
        self.sem_names += 1
        return self.nc.alloc_semaphore(name="s%d_%s" % (self.sem_names, name))

    def _add(self, ins, reads, writes):
        deps = ins.deps
        for b in reads:
            if b.w is not None:
                deps.append(b.w)
        for b in writes:
            for r in b.r.values():
                deps.append(r)
            if b.w is not None:
                deps.append(b.w)
        ph = self.phase
        nd = []
        for d in deps:
            if d.phase != ph or d is ins:
                continue
            if d.is_dma:
                k = d.dkey
                ins.dwaits[k] = self.dma_cnt[k]
            elif d.eng == ins.eng and not ins.is_dma and d not in \
                    [b.w for b in reads]:
                continue
            else:
                nd.append(d)
        ins.deps = nd
        for b in reads:
            if ins.is_dma:
                b.r[("dma", ins.n)] = ins
            else:
                b.r[ins.eng] = ins
        for b in writes:
            b.w = ins
            b.r = {}
        self.lists[ins.eng].append(ins)
        self.stats[ins.eng] += 1

    def op(self, eng, fn, reads=(), writes=()):
        self.n += 1
        ins = Ins(eng, fn, False, self.phase, self.n)
        self._add(ins, reads, writes)
        return ins

    def dma(self, out, in_, key, reads=(), writes=(), q="sp", **kw):
        self.n += 1
        ins = Ins(q, (lambda e: e.dma_start(out=out, in_=in_, **kw)), True,
                  self.phase, self.n)
        ins.dkey = key
        if key not in self.dma_sem:
            self.dma_sem[key] = self._newsem(key)
            self.dma_cnt[key] = 0
        self._add(ins, reads, writes)
        self.dma_cnt[key] += 16
        return ins

    def _prog_sem(self, eng, epoch):
        lst = self.prog_sems[eng]
        while len(lst) <= epoch:
            lst.append(self._newsem("p_%s%d" % (eng, len(lst))))
        return lst[epoch]

    def end_phase(self, skip_prefix=None):
        nc = self.nc
        lists = self.lists
        for e in ENGS:
            for ins in lists[e]:
                for d in ins.deps:
                    if d.eng == "pe" and ins.eng == "pe" and not ins.is_dma:
                        continue
                    d.marked = True
        last_compute = {}
        for e in ENGS:
            for ins in reversed(lists[e]):
                if not ins.is_dma:
                    ins.marked = True
                    last_compute[e] = ins
                    break
        for e in ENGS:
            for ins in lists[e]:
                if ins.marked and not ins.is_dma:
                    c = self.nmarks[e]
                    self.nmarks[e] = c + 1
                    ins.sem = self._prog_sem(e, c // EPOCH)
                    ins.val = (c % EPOCH) + 1
        dma_final = {k: v for k, v in self.dma_cnt.items()
                     if not (skip_prefix and k.startswith(skip_prefix))}

        def emit(eng_name, eng):
            waited = self.waited[eng_name]

            def wait(sem, val):
                key = id(sem)
                if waited.get(key, 0) >= val:
                    return
                eng.wait_ge(sem, val)
                waited[key] = val

            for ins in lists[eng_name]:
                for d in ins.deps:
                    if d.eng == "pe" and eng_name == "pe" and not ins.is_dma:
                        continue
                    wait(d.sem, d.val)
                for k, v in ins.dwaits.items():
                    wait(self.dma_sem[k], v)
                bi = ins.fn(eng)
                if ins.is_dma:
                    bi.then_inc(self.dma_sem[ins.dkey], 16)
                elif ins.marked:
                    bi.then_inc(ins.sem, 1)
            for o, li in last_compute.items():
                if o != eng_name:
                    wait(li.sem, li.val)
            for k, v in dma_final.items():
                if v > 0:
                    wait(self.dma_sem[k], v)

        with nc.Block() as block:
            @block.tensor
            def _(eng):
                emit("pe", eng)

            @block.scalar
            def _(eng):
                emit("act", eng)

            @block.vector
            def _(eng):
                emit("dve", eng)

            @block.gpsimd
            def _(eng):
                emit("pool", eng)

            @block.sync
            def _(eng):
                emit("sp", eng)

        self.lists = {e: [] for e in ENGS}
        self.phase += 1


class Ring:
    def __init__(self, items):
        self.items = items
        self.i = 0

    def next(self):
        it = self.items[self.i % len(self.items)]
        self.i += 1
        return it


_CONST_CACHE = {}


def _dft_table(S):
    n = S // 128
    s = np.arange(S, dtype=np.int64)
    k = np.arange(S // 2, dtype=np.int64)
    prod = (s[:, None] * k[None, :]) % S
    ang = prod.astype(np.float64) * (2.0 * np.pi / S)
    c = np.cos(ang)
    ns = -np.sin(ang)

    def lay(m):
        m = np.concatenate([m[0::2], m[1::2]], axis=0)
        m = m.reshape(n, 128, n // 2, 128)
        return np.ascontiguousarray(m.transpose(2, 1, 0, 3)).astype(bf).reshape(n // 2, 128, n * 128)
    return lay(c), lay(ns)


def _consts(S_set, smax):
    key = (tuple(sorted(S_set)), smax)
    if key in _CONST_CACHE:
        return _CONST_CACHE[key]
    c = {}
    c["ident"] = np.eye(128, dtype=np.float32).astype(bf)
    rot = np.zeros((128, 128), np.float32)
    for d in range(128):
        dl = d % 64
        base = d - dl
        if dl < 32:
            rot[base + dl + 32, d] = -1.0
        else:
            rot[base + dl - 32, d] = 1.0
    c["rotm"] = rot.astype(bf)
    i = np.arange(128, dtype=np.int64)
    a = (i[:, None] * i[None, :]) % 128
    ang = a.astype(np.float64) * (2 * np.pi / 128)
    c["dft128"] = np.concatenate([np.cos(ang), np.sin(ang)], axis=1).astype(bf)
    inv_freq = (1.0 / (np.float32(10000.0) ** (np.arange(0, HD, 2, dtype=np.float32) / np.float32(HD)))).astype(np.float32)
    pos = np.arange(smax, dtype=np.float32)
    angr = (pos[:, None] * inv_freq[None, :]).astype(np.float32)
    angr = np.concatenate([angr, angr], axis=1)
    cosT = np.cos(angr).astype(np.float32).T
    sinT = np.sin(angr).astype(np.float32).T
    c["ropec"] = np.ascontiguousarray(np.concatenate([cosT, cosT], axis=0))
    c["ropes"] = np.ascontiguousarray(np.concatenate([sinT, sinT], axis=0))
    z = np.zeros_like(cosT)
    c["ropec_lo"] = np.ascontiguousarray(np.concatenate([cosT, z], axis=0))
    c["ropes_lo"] = np.ascontiguousarray(np.concatenate([sinT, z], axis=0))
    c["ropec_hi"] = np.ascontiguousarray(np.concatenate([z, cosT], axis=0))
    c["ropes_hi"] = np.ascontiguousarray(np.concatenate([z, sinT], axis=0))
    j = np.arange(128)[:, None]
    q = np.arange(128)[None, :]
    mp = (q <= j).astype(np.float32)
    mn = (j <= q).astype(np.float32)
    c["maskp"] = np.tile(mp, (1, 4)).astype(bf)
    c["maskn"] = np.tile(mn, (1, 4)).astype(bf)
    for S in S_set:
        tc, ts = _dft_table(S)
        c["tabc%d" % S] = tc
        c["tabs%d" % S] = ts
    _CONST_CACHE[key] = c
    return c


PW_IN, PW_OUT, PW_UP, PW_DOWN, NPIECE = 0, 6, 10, 26, 42


import os
PARTS = set(os.environ.get('P1PARTS', 'four,qk,v,att').split(','))
ATTL = int(os.environ.get('ATTL', '3'))
NOMASK = int(os.environ.get('NOMASK', '0'))
NOEXP = int(os.environ.get('NOEXP', '0'))


def build(S_list, debug=False, upto=3):
    nc = bass.Bass("TRN2", target_bir_lowering=False)
    NTOK = sum(S_list)
    S_set = sorted(set(S_list))
    SMAX = max(S_list)

    def din(name, shape, dt=F32):
        return nc.dram_tensor(name, list(shape), dt, kind="ExternalInput").ap()

    x = din("x", [NTOK, D])
    w_in = din("w_in", [D, 2560])
    w_f = din("w_fourier", [8, 128, 128])
    sink = din("attn_sink", [NH])
    g_mix = din("ln_mix_g", [D])
    g_f = din("out_norm_fourier_g", [1024])
    g_a = din("out_norm_attn_g", [1024])
    w_out = din("w_out", [D, D])
    g_mlp = din("ln_mlp_g", [D])
    w_up = din("w_up", [D, DFF])
    w_down = din("w_down", [DFF, D])
    g_fin = din("ln_final_g", [D])
    c_ident = din("ident", [128, 128], BF16)
    c_rotm = din("rotm", [128, 128], BF16)
    c_dft = din("dft128", [128, 256], BF16)
    c_ropec = din("ropec", [128, SMAX])
    c_ropes = din("ropes", [128, SMAX])
    c_rope_lh = [din(n, [128, SMAX]) for n in ("ropec_lo", "ropes_lo", "ropec_hi", "ropes_hi")]
    c_maskp = din("maskp", [128, 512], BF16)
    c_maskn = din("maskn", [128, 512], BF16)
    tabs = {}
    for S in S_set:
        n = S // 128
        tabs[S] = (din("tabc%d" % S, [n // 2, 128, n * 128], BF16),
                   din("tabs%d" % S, [n // 2, 128, n * 128], BF16))
    y = nc.dram_tensor("y", [NTOK, D], F32, kind="ExternalOutput").ap()
    skind = "ExternalOutput" if debug else "Internal"
    wp = nc.dram_tensor("wp_scr", [NPIECE, 128, 8192], BF16, kind="Internal").ap()
    xab = nc.dram_tensor("xab_scr", [NTOK, 2048], BF16, kind=skind).ap()
    mixed = nc.dram_tensor("mixed_scr", [NTOK, 2048], BF16, kind=skind).ap()

    P = Prog(nc)

    with ExitStack() as ges:
        def gsb(name, shape, dt):
            return ges.enter_context(nc.sbuf_tensor(name, list(shape), dt))

        ident = gsb("ident_sb", [128, 128], BF16)
        rotm = gsb("rotm_sb", [128, 128], BF16)
        maskp = gsb("maskp_sb", [128, 512], BF16)
        maskn = gsb("maskn_sb", [128, 512], BF16)
        AB = gsb("AB_sb", [128, 8, 256], BF16)
        esink = gsb("esink_sb", [128, NH], F32)
        epsb = gsb("eps_sb", [128, 1], F32)
        b_const = Buf("const")
        b_AB = Buf("AB")
        b_esink = Buf("esink")

        with ExitStack() as es:
            def sb(name, shape, dt):
                return es.enter_context(nc.sbuf_tensor(name, list(shape), dt))

            def ps(name):
                return es.enter_context(nc.psum_tensor(name, [128, 512], F32))

            banks = [ps("p0bank%d" % i) for i in range(2)]
            b_banks = [Buf("p0bank%d" % i) for i in range(2)]
            dft = sb("dft_sb", [128, 256], BF16)
            b_dft = Buf("dft")
            gcol = sb("gcol", [128, 3, 16], F32)
            b_gcol = Buf("gcol")
            wf_st = sb("wf_st", [128, 8, 128], F32)
            wf_b = sb("wf_b", [128, 8, 128], BF16)
            b_wfst = Buf("wfst")
            b_wfb = Buf("wfb")
            stage = [sb("stage%d" % i, [128, 16, 512], F32) for i in range(2)]
            b_stage = [Buf("stage%d" % i) for i in range(2)]
            outb = [sb("outb%d" % i, [128, 16, 512], BF16) for i in range(2)]
            b_outb = [Buf("outb%d" % i) for i in range(2)]

            P.op("pool", lambda e: e.memset(epsb[:], EPS), writes=[b_const])
            P.dma(ident[:], c_ident, "c0", writes=[b_const])
            P.dma(rotm[:], c_rotm, "c0", writes=[b_const])
            P.dma(maskp[:], c_maskp, "c0", writes=[b_const])
            P.dma(maskn[:], c_maskn, "c0", writes=[b_const])
            P.dma(dft[:], c_dft, "c1", writes=[b_dft])
            P.dma(esink[:], sink.partition_broadcast(128), "c2", writes=[b_esink])
            P.op("act", lambda e: e.activation(out=esink[:], in_=esink[:], func=AF.Exp),
                 reads=[b_esink], writes=[b_esink])
            P.dma(gcol[:, 0, :], g_mix.rearrange("(c p) -> p c", p=128), "c3",
                  writes=[b_gcol], allow_slow_non_contiguous=True)
            P.dma(gcol[:, 1, 0:8], g_f.rearrange("(c p) -> p c", p=128), "c3",
                  writes=[b_gcol], allow_slow_non_contiguous=True)
            P.dma(gcol[:, 1, 8:16], g_a.rearrange("(c p) -> p c", p=128), "c3",
                  writes=[b_gcol], allow_slow_non_contiguous=True)
            P.dma(gcol[:, 2, :], g_mlp.rearrange("(c p) -> p c", p=128), "c3",
                  writes=[b_gcol], allow_slow_non_contiguous=True)
            P.dma(wf_st[:], w_f.rearrange("g d e -> d g e"), "c4", writes=[b_wfst])
            P.op("dve", lambda e: e.tensor_copy(wf_b[:], wf_st[:]), reads=[b_wfst], writes=[b_wfb])
            for g in range(8):
                bk = banks[g % 2]
                bb = b_banks[g % 2]
                P.op("pe", lambda e, bk=bk, g=g: e.matmul(bk[:, 0:128], lhsT=dft[:, 0:128], rhs=wf_b[:, g, :],
                                                          start=True, stop=True),
                     reads=[b_dft, b_wfb], writes=[bb])
                P.op("pe", lambda e, bk=bk, g=g: e.matmul(bk[:, 128:256], lhsT=dft[:, 128:256], rhs=wf_b[:, g, :],
                                                          start=True, stop=True),
                     reads=[b_dft, b_wfb], writes=[bb])
                P.op("act", lambda e, bk=bk, g=g: e.activation(out=AB[:, g, :], in_=bk[:, 0:256], func=AF.Copy),
                     reads=[bb], writes=[b_AB])

            pieces = []

            def rows(wap, r0):
                return wap[r0:r0 + 2048, :]

            for i in range(4):
                pieces.append(([(0, w_in[:, i * 512:(i + 1) * 512])], 0, 512))
            segs = []
            for kvh in range(4):
                src = w_in[:, 2048 + kvh * 64:2048 + (kvh + 1) * 64]
                segs.append((kvh * 128, src))
                segs.append((kvh * 128 + 64, src))
            pieces.append((segs, 0, 512))
            pieces.append(([(0, w_in[:, 2304:2560])], 0, 256))
            for c in range(4):
                pieces.append(([(0, w_out[:, c * 512:(c + 1) * 512])], 1, 512))
            for i in range(16):
                pieces.append(([(0, w_up[:, i * 512:(i + 1) * 512])], None, 512))
            for c in range(4):
                for fg in range(4):
                    pieces.append(([(0, w_down[fg * 2048:(fg + 1) * 2048, c * 512:(c + 1) * 512])], None, 512))
            assert len(pieces) == NPIECE
            bg_pieces = pieces[PW_UP:]
            for pi in range(PW_OUT):
                segs, gsel, width = pieces[pi]
                for (d0, src) in segs:
                    wseg = src.shape[1]
                    P.dma(wp[pi].rearrange("p (rc c) -> p rc c", rc=16)[:, :, d0:d0 + wseg],
                          src.rearrange("(rc p) c -> p rc c", p=128), "cin%d" % (pi % 4), q="pool")
            for pi, (segs, gsel, width) in list(enumerate(pieces[:PW_UP]))[PW_OUT:]:
                st = stage[pi % 2]
                bs = b_stage[pi % 2]
                ob = outb[pi % 2]
                bo = b_outb[pi % 2]
                for (d0, src) in segs:
                    wseg = src.shape[1]
                    P.dma(st[:, :, d0:d0 + wseg], src.rearrange("(rc p) c -> p rc c", p=128),
                          "ldst%d" % (pi % 2), writes=[bs])
                eng = ("dve", "pool")[pi % 2]
                if gsel is None:
                    P.op(eng, lambda e, st=st, ob=ob, width=width: e.tensor_copy(ob[:, :, 0:width], st[:, :, 0:width]),
                         reads=[bs], writes=[bo])
                else:
                    P.op(eng, lambda e, st=st, ob=ob, width=width, gsel=gsel: e.tensor_tensor(
                        ob[:, :, 0:width], st[:, :, 0:width],
                        gcol[:, gsel, :].unsqueeze(2).to_broadcast([128, 16, width]), ALU.mult),
                         reads=[bs, b_gcol], writes=[bo])
                P.dma(wp[pi].rearrange("p (rc c) -> p rc c", rc=16)[:, :, 0:width], ob[:, :, 0:width],
                      "stw%d" % (pi % 2), reads=[bo], q="pool")
            P.end_phase()
            if upto == 0:
                return nc, P

        class WStream:
            def __init__(self, slots, bufs, seq, tag):
                self.slots = slots
                self.bufs = bufs
                self.seq = seq
                self.issued = 0
                self.i = 0
                self.tag = tag

            def _issue(self, k):
                n = len(self.slots)
                pidx, width = self.seq[k]
                sl = self.slots[k % n]
                P.dma(sl[:, :, 0:width], wp[pidx].rearrange("p (rc c) -> p rc c", rc=16)[:, :, 0:width],
                      "%s%d" % (self.tag, k % n), writes=[self.bufs[k % n]])

            def get(self):
                n = len(self.slots)
                k = self.i
                while self.issued < min(len(self.seq), k + n):
                    self._issue(self.issued)
                    self.issued += 1
                self.i += 1
                return self.slots[k % n], self.bufs[k % n]

        seq_base = []
        b = 0
        for S in S_list:
            seq_base.append(b)
            b += S

        with ExitStack() as es:
            def sb(name, shape, dt):
                return es.enter_context(nc.sbuf_tensor(name, list(shape), dt))

            pbanks = [es.enter_context(nc.psum_tensor("p1bank%d" % i, [128, 512], F32)) for i in range(8)]
            pb = [Buf("p1bank%d" % i) for i in range(8)]
            TR = Ring([(pbanks[i][:].bitcast(BF16), pb[i]) for i in (0, 1)])
            G = Ring([(pbanks[i], pb[i]) for i in (2, 3, 4)])
            OB = [pbanks[5], pbanks[6], pbanks[7]]
            b_OB = [pb[5], pb[6], pb[7]]

            wslots = [sb("w1s%d" % i, [128, 16, 512], BF16) for i in range(3)]
            wbufs = [Buf("w1s%d" % i) for i in range(3)]
            xsr = Ring([(sb("xs%d" % i, [128, D], F32), Buf("xs%d" % i), i) for i in range(2)])
            junk = sb("junk", [128, D], BF16)
            b_junk = Buf("junk")
            hbr = Ring([(sb("hb%d" % i, [128, D], BF16), Buf("hb%d" % i)) for i in range(4)])
            hT = sb("hT", [128, 16, 512], BF16)
            b_hT = Buf("hT")
            zfT = sb("zfT", [128, 8, 512], BF16)
            b_zf = [Buf("zf%d" % g) for g in range(8)]
            xabr = Ring([(sb("xabsb%d" % i, [128, 2048], BF16), Buf("xabsb%d" % i), i) for i in range(2)])
            qT = [sb("qT%d" % i, [128, 8, 512], BF16) for i in range(2)]
            b_qT = [Buf("qT%d" % i) for i in range(2)]
            kT = [sb("kT%d" % i, [128, 2, 4, 512], BF16) for i in range(2)]
            b_kT = [Buf("kT%d" % i) for i in range(2)]
            v1 = [sb("v1_%d" % i, [128, 4, 4 * 66], BF16) for i in range(2)]
            b_v1 = [Buf("v1_%d" % i) for i in range(2)]
            ropr = Ring([(sb("rc%d" % i, [128, 512], F32), sb("rs%d" % i, [128, 512], F32),
                          Buf("rc%d" % i), Buf("rs%d" % i), i) for i in range(1)])
            roplh = Ring([([sb("rlh%d_%d" % (i, t), [128, 512], F32) for t in range(4)], Buf("rlh%d" % i), i)
                          for i in range(1)])
            qbr = Ring([(sb("qb%d" % i, [128, 512], BF16), Buf("qb%d" % i)) for i in range(2)])
            t1r = Ring([(sb("t1_%d" % i, [128, 512], F32), Buf("t1_%d" % i)) for i in range(2)])
            t2r = Ring([(sb("t2_%d" % i, [128, 512], F32), Buf("t2_%d" % i)) for i in range(2)])
            pTr = Ring([(sb("pT%d" % i, [128, 512], BF16), Buf("pT%d" % i)) for i in range(6)])
            den = sb("den", [128, NH], F32)
            b_den = [Buf("den%d" % i) for i in range(3)]
            oar = Ring([(sb("oa%d" % i, [128, 1024], F32), Buf("oa%d" % i)) for i in range(2)])
            oanr = Ring([(sb("oan%d" % i, [128, 1024], BF16), Buf("oan%d" % i), i) for i in range(2)])
            ssr = Ring([(sb("ss%d" % i, [128, 2], F32), Buf("ss%d" % i)) for i in range(4)])

            gmix = sb("gmix_sb", [128, D], F32)
            b_gmix = Buf("gmix")
            P.dma(gmix[:], g_mix.partition_broadcast(128), "c6", writes=[b_gmix])

            def norm_to_bf16(src, b_src, dst, b_dst, width, gain=None):
                ss, b_ss = ssr.next()
                P.op("act", lambda e: e.activation(out=junk[:, 0:width], in_=src, func=AF.Square,
                                                   accum_out=ss[:, 0:1]),
                     reads=[b_src], writes=[b_junk, b_ss])
                P.op("act", lambda e: e.activation(out=ss[:, 1:2], in_=ss[:, 0:1], func=AF.Ln, scale=1.0 / width,
                                                   bias=epsb[:, 0:1]),
                     reads=[b_ss, b_const], writes=[b_ss])
                P.op("act", lambda e: e.activation(out=ss[:, 1:2], in_=ss[:, 1:2], func=AF.Exp, scale=-0.5),
                     reads=[b_ss], writes=[b_ss])
                if gain is not None:
                    P.op("dve", lambda e: e.scalar_tensor_tensor(dst, src, ss[:, 1:2], gain, ALU.mult, ALU.mult),
                         reads=[b_src, b_ss, b_gmix], writes=[b_dst])
                else:
                    P.op("dve", lambda e: e.tensor_tensor(dst, src, ss[:, 1:2].to_broadcast([128, width]), ALU.mult),
                         reads=[b_src, b_ss], writes=[b_dst])

            for i in range(2):
                P.op("pool", lambda e, i=i: e.memset(v1[i][:], 1.0), writes=[b_v1[i]])

            seq1 = []
            for S in S_list:
                for T in range(S // 512):
                    seq1 += [(PW_IN + i, 512) for i in range(5)] + [(PW_IN + 5, 256)]
            ws = WStream(wslots, wbufs, seq1, "w1_")
            evq = [0]

            def evac_eng():
                evq[0] += 1
                return ("dve", "act")[evq[0] % 2]

            def copy_op(eng, out, in_, reads, writes):
                if eng == "act":
                    P.op("act", lambda e: e.activation(out=out, in_=in_, func=AF.Copy), reads=reads, writes=writes)
                else:
                    P.op(eng, lambda e: e.tensor_copy(out, in_), reads=reads, writes=writes)

            def sc_tiles(bq, kvh, NB):
                qs = (bq // 4) % 2
                qo = (bq % 4) * 128
                pts = []
                for j in (bq - 1, bq, bq + 1):
                    if j < 0 or j >= NB:
                        continue
                    ks = (j // 4) % 2
                    ko = (j % 4) * 128
                    bank, bbank = G.next()
                    for g in range(4):
                        pair = 2 * kvh + g // 2
                        hf = g % 2
                        P.op("pe", lambda e, bank=bank, g=g, ks=ks, ko=ko, hf=hf, pair=pair, kvh=kvh, qs=qs, qo=qo: e.matmul(
                            bank[:, g * 128:(g + 1) * 128],
                            lhsT=kT[ks][:, hf, kvh, ko:ko + 128],
                            rhs=qT[qs][:, pair, qo:qo + 128],
                            start=True, stop=True),
                             reads=[b_kT[ks], b_qT[qs]], writes=[bbank])
                    pt, b_pt = pTr.next()
                    P.op("act", lambda e, pt=pt, bank=bank: e.activation(out=pt[:], in_=bank[:], func=AF.Exp,
                                                                         scale=HD ** -0.5),
                         reads=[bbank], writes=[b_pt])
                    if j != bq:
                        mk = maskp if j < bq else maskn
                        P.op("dve", lambda e, pt=pt, mk=mk: e.tensor_tensor(pt[:], pt[:], mk[:], ALU.mult),
                             reads=[b_pt], writes=[b_pt])
                    pts.append((pt, b_pt, j))
                return pts

            def pv(kvh, pts):
                for g in range(4):
                    h = 4 * kvh + g
                    ob = OB[h // 7]
                    bob = b_OB[h // 7]
                    oc = (h % 7) * 66
                    for idx, (pt, b_pt, j) in enumerate(pts):
                        ks = (j // 4) % 2
                        jb = j % 4
                        P.op("pe", lambda e, ob=ob, oc=oc, pt=pt, g=g, ks=ks, jb=jb, idx=idx, n=len(pts), kvh=kvh: e.matmul(
                            ob[:, oc:oc + 66], lhsT=pt[:, g * 128:(g + 1) * 128],
                            rhs=v1[ks][:, jb, kvh * 66:(kvh + 1) * 66],
                            start=(idx == 0), stop=(idx == n - 1)),
                             reads=[b_pt, b_v1[ks]], writes=[bob])

            OBH = ((0, 7), (7, 7), (14, 2))

            def norm_bank(bk, oa, b_oa):
                h0, nh = OBH[bk]
                dn = den[:, h0:h0 + nh]
                P.op("dve", lambda e: e.tensor_tensor(
                    dn, OB[bk][:, 0:nh * 66].rearrange("p (h e) -> p h e", e=66)[:, :, 64],
                    esink[:, h0:h0 + nh], ALU.add),
                     reads=[b_OB[bk], b_esink], writes=[b_den[bk]])
                P.op("dve", lambda e: e.reciprocal(dn, dn), reads=[b_den[bk]], writes=[b_den[bk]])
                P.op("dve", lambda e: e.tensor_tensor(
                    oa[:, h0 * 64:(h0 + nh) * 64].rearrange("p (h e) -> p h e", e=64),
                    OB[bk][:, 0:nh * 66].rearrange("p (h e) -> p h e", e=66)[:, :, 0:64],
                    dn.unsqueeze(2).to_broadcast([128, nh, 64]), ALU.mult),
                     reads=[b_OB[bk], b_den[bk]], writes=[b_oa])

            pending = []

            def flush_pending():
                while pending:
                    pending.pop(0)[1]()

            def attention_stream(blocks, NB, tb):
                groups = [(bq, kvh) for bq in blocks for kvh in range(4)]
                if not groups:
                    return
                oas = {}
                nxt = sc_tiles(groups[0][0], groups[0][1], NB)
                for i, (bq, kvh) in enumerate(groups):
                    cur = nxt
                    if i + 1 < len(groups):
                        nxt = sc_tiles(groups[i + 1][0], groups[i + 1][1], NB)
                    if pending:
                        pending.pop(0)[1]()
                    if kvh == 0:
                        oas[bq] = oar.next()
                    oa, b_oa = oas[bq]
                    pv(kvh, cur)
                    if kvh == 1:
                        norm_bank(0, oa, b_oa)
                    if kvh == 3:
                        norm_bank(1, oa, b_oa)
                        norm_bank(2, oa, b_oa)

                        def fin(oa=oa, b_oa=b_oa, bq=bq):
                            oan, b_oan, oi = oanr.next()
                            norm_to_bf16(oa[:], b_oa, oan[:], b_oan, 1024)
                            r0 = tb + bq * 128
                            P.dma(mixed[r0:r0 + 128, 1024:2048], oan[:], "stoan%d" % oi, reads=[b_oan], q="pool")
                        pending.append(("fin", fin))

            def x_load(tok0, j):
                xs, b_xs, xi = xsr.next()
                P.dma(xs[:], x[tok0 + j * 128:tok0 + (j + 1) * 128, :], "ldxs%d" % xi, writes=[b_xs])
                return xs, b_xs

            def prep_sub(tok0, j, hbs, xq=None):
                if xq is None:
                    xs, b_xs = x_load(tok0, j)
                else:
                    xs, b_xs = xq.pop(0)
                hb, b_hb = hbr.next()
                norm_to_bf16(xs[:], b_xs, hb[:], b_hb, D, gain=gmix[:])
                hbs.append((hb, b_hb))
                if xq is not None and j + 2 < 4:
                    xq.append(x_load(tok0, j + 2))

            def prep_tile(tok0):
                hbs = []
                for j in range(4):
                    prep_sub(tok0, j, hbs)
                return hbs

            def rope_start(bank, bbank, outs):
                qb, b_qb = qbr.next()
                tok = Buf("tok")
                copy_op("act", qb[:], bank[:], [bbank], [b_qb, tok])
                return (bank, bbank, outs, qb, b_qb, tok)

            def rope_finish(st):
                bank, bbank, outs, qb, b_qb, tok = st
                bank2, bbank2 = pbanks[1], pb[1]
                P.op("pe", lambda e: e.matmul(bank2[:], lhsT=rotm[:], rhs=qb[:], start=True, stop=True),
                     reads=[b_qb, b_const], writes=[bbank2])
                for (dst, b_dst, rc, rs, b_rc, b_rs) in outs:
                    rope_out(bank, bbank, bank2, bbank2, tok, dst, b_dst, rc, rs, b_rc, b_rs)

            def rope_out(bank, bbank, bank2, bbank2, tok, dst, b_dst, rc, rs, b_rc, b_rs):
                t1, b_t1 = t1r.next()
                t2, b_t2 = t2r.next()
                P.op("dve", lambda e: e.tensor_tensor(t1[:], bank[:], rc[:], ALU.mult),
                     reads=[bbank, b_rc, tok], writes=[b_t1])
                P.op("dve", lambda e: e.tensor_tensor(t2[:], bank2[:], rs[:], ALU.mult),
                     reads=[bbank2, b_rs], writes=[b_t2])
                P.op("pool", lambda e: e.tensor_tensor(dst, t1[:], t2[:], ALU.add),
                     reads=[b_t1, b_t2], writes=[b_dst])

            tiles = []
            for si, S in enumerate(S_list):
                for T in range(S // 512):
                    tiles.append((si, T))
            bg_todo = list(range(len(bg_pieces)))
            bg_per_tile = -(-len(bg_todo) // len(tiles))

            def bg_issue(n):
                for _ in range(n):
                    if not bg_todo:
                        return
                    bi = bg_todo.pop(0)
                    segs, gsel, width = bg_pieces[bi]
                    (d0, src), = segs
                    P.dma(wp[PW_UP + bi].rearrange("p (rc c) -> p rc c", rc=16),
                          src.rearrange("(rc p) c -> p rc c", p=128), "bg%d" % (bi % 4), q="pool")
            hbs_next = prep_tile(seq_base[0])
            att_next = 0
            for ti, (si, T) in enumerate(tiles):
                S = S_list[si]
                tb = seq_base[si]
                NT = S // 512
                NB = S // 128
                if T == 0:
                    att_next = 0
                tok0 = tb + T * 512
                ts_ = T % 2
                hbs = hbs_next
                rc, rs, b_rc, b_rs, ri = ropr.next()
                P.dma(rc[:], c_ropec[:, T * 512:(T + 1) * 512], "ldrc%d" % ri, writes=[b_rc])
                P.dma(rs[:], c_ropes[:, T * 512:(T + 1) * 512], "ldrs%d" % ri, writes=[b_rs])
                rlh, b_rlh, rli = roplh.next()
                for t in range(4):
                    P.dma(rlh[t][:], c_rope_lh[t][:, T * 512:(T + 1) * 512], "ldrlh%d" % rli, writes=[b_rlh])
                for j in range(4):
                    hb, b_hb = hbs[j]
                    for half in range(2):
                        trv, b_tr = TR.next()
                        for kk in range(8):
                            kc = half * 8 + kk
                            P.op("pe", lambda e, trv=trv, kk=kk, kc=kc, hb=hb: e.transpose(
                                trv[:, kk * 128:(kk + 1) * 128], hb[:, kc * 128:(kc + 1) * 128], ident[:]),
                                 reads=[b_hb, b_const], writes=[b_tr])
                        copy_op(evac_eng(), hT[:, half * 8:(half + 1) * 8, j * 128:(j + 1) * 128],
                                trv[:, 0:1024].rearrange("p (a b) -> p a b", a=8), [b_tr], [b_hT])
                for pc in range(2):
                    wsl, b_w = ws.get()
                    for gi in range(4):
                        g = pc * 4 + gi
                        bank, bbank = G.next()
                        for kc in range(16):
                            P.op("pe", lambda e, bank=bank, wsl=wsl, kc=kc, gi=gi: e.matmul(
                                bank[:], lhsT=wsl[:, kc, gi * 128:(gi + 1) * 128], rhs=hT[:, kc, :],
                                start=(kc == 0), stop=(kc == 15)),
                                 reads=[b_w, b_hT], writes=[bbank])
                        copy_op("act", zfT[:, g, :], bank[:], [bbank], [b_zf[g]])
                for j in range(4):
                    xsb, b_xsb, xbi = xabr.next()
                    for gp in range(4):
                        bank, bbank = G.next()
                        for gg in range(2):
                            g = gp * 2 + gg
                            P.op("pe", lambda e, bank=bank, gg=gg, g=g, j=j: e.matmul(
                                bank[:, gg * 256:(gg + 1) * 256], lhsT=zfT[:, g, j * 128:(j + 1) * 128],
                                rhs=AB[:, g, :], start=True, stop=True),
                                 reads=[b_zf[g], b_AB], writes=[bbank])
                        hf = gp // 2
                        q0 = (2 * gp) % 4
                        dst = xsb[:].rearrange("p (h a q e) -> p h a q e", h=2, a=2, q=4)[:, hf, :, q0:q0 + 2, :]
                        src = bank[:].rearrange("p (g a e) -> p a g e", g=2, a=2)
                        copy_op("dve", dst, src, [bbank], [b_xsb])
                    r0 = tok0 + j * 128
                    P.dma(xab[r0:r0 + 128, :], xsb[:], "stxab%d" % xbi, reads=[b_xsb], q="pool")
                bg_issue(bg_per_tile)
                rope_prev = None
                for pc in range(2, 5):
                    wsl, b_w = ws.get()
                    for ci in range(4):
                        bank, bbank = G.next()
                        for kc in range(16):
                            P.op("pe", lambda e, bank=bank, wsl=wsl, kc=kc, ci=ci: e.matmul(
                                bank[:], lhsT=wsl[:, kc, ci * 128:(ci + 1) * 128], rhs=hT[:, kc, :],
                                start=(kc == 0), stop=(kc == 15)),
                                 reads=[b_w, b_hT], writes=[bbank])
                        if pc < 4:
                            p = (pc - 2) * 4 + ci
                            st = rope_start(bank, bbank, [(qT[ts_][:, p, :], b_qT[ts_], rc, rs, b_rc, b_rs)])
                        else:
                            st = rope_start(bank, bbank,
                                            [(kT[ts_][:, 0, ci, :], b_kT[ts_], rlh[0], rlh[1], b_rlh, b_rlh),
                                             (kT[ts_][:, 1, ci, :], b_kT[ts_], rlh[2], rlh[3], b_rlh, b_rlh)])
                        if rope_prev is not None:
                            rope_finish(rope_prev)
                        rope_prev = st
                rope_finish(rope_prev)
                wsl, b_w = ws.get()
                for j in range(4):
                    bank, bbank = G.next()
                    for kc in range(16):
                        P.op("pe", lambda e, bank=bank, wsl=wsl, kc=kc, j=j: e.matmul(
                            bank[:, 0:256], lhsT=hT[:, kc, j * 128:(j + 1) * 128], rhs=wsl[:, kc, 0:256],
                            start=(kc == 0), stop=(kc == 15)),
                             reads=[b_w, b_hT], writes=[bbank])
                    copy_op("act", v1[ts_][:, j, :].rearrange("p (h e) -> p h e", e=66)[:, :, 0:64],
                            bank[:, 0:256].rearrange("p (h e) -> p h e", e=64), [bbank], [b_v1[ts_]])
                preps = []
                if ti + 1 < len(tiles):
                    nsi, nT = tiles[ti + 1]
                    hbs_next = []
                    ntok0 = seq_base[nsi] + nT * 512
                    xq = [x_load(ntok0, 0), x_load(ntok0, 1)]
                    preps = [(lambda j=j, ntok0=ntok0, hl=hbs_next, xq=xq: prep_sub(ntok0, j, hl, xq)) for j in range(4)]
                last = 4 * T + 2 if T < NT - 1 else NB - 1
                for p in preps:
                    pending.append(("prep", p))
                preps = []
                attention_stream(list(range(att_next, last + 1)), NB, tb)
                att_next = last + 1
                keep = []
                for kind, p in pending:
                    if kind == "prep":
                        p()
                    else:
                        keep.append((kind, p))
                del pending[:]
                pending.extend(keep[-1:])
                for kind, p in keep[:-1]:
                    p()
            flush_pending()
            bg_issue(len(bg_todo))
            P.end_phase(skip_prefix="bg")
            if upto == 1:
                return nc, P

        with ExitStack() as es:
            def sb(name, shape, dt):
                return es.enter_context(nc.sbuf_tensor(name, list(shape), dt))

            NSCM = SMAX // 128
            pbanks = [es.enter_context(nc.psum_tensor("p2bank%d" % i, [128, 512], F32)) for i in range(4)]
            pb2 = [Buf("p2bank%d" % i) for i in range(4)]
            PAIR = Ring([((pbanks[0], pb2[0]), (pbanks[1], pb2[1])), ((pbanks[2], pb2[2]), (pbanks[3], pb2[3]))])
            xr = sb("xabr", [128, NSCM, 1024], BF16)
            b_xr = [Buf("xabr%d" % i) for i in range(NSCM)]
            tabr = Ring([(sb("tc%d" % i, [128, NSCM * 128], BF16), sb("tn%d" % i, [128, NSCM * 128], BF16),
                          Buf("tc%d" % i), Buf("tn%d" % i), i) for i in range(3)])
            esr = Ring([(sb("es%d" % i, [128, 512], F32), Buf("es%d" % i)) for i in range(2)])
            ofr = Ring([(sb("of%d" % i, [128, 512], BF16), Buf("of%d" % i), i) for i in range(4)])
            for si, S in enumerate(S_list):
                tb = seq_base[si]
                NSC = S // 128
                NH2 = NSC // 2
                gsz = 4 if NH2 >= 4 else NH2
                tcd, tsd = tabs[S]
                scale = float(1.0 / np.sqrt(S * 128.0))
                for hf in range(2):
                    xv = xab[tb:tb + S, hf * 1024:(hf + 1) * 1024].rearrange("(m two) c -> two m c", two=2)
                    for r in range(2):
                        for grp in range(NH2 // gsz):
                            c0 = r * NH2 + grp * gsz
                            P.dma(xr[:, c0:c0 + gsz, :],
                                  xv[r][grp * gsz * 128:(grp + 1) * gsz * 128].rearrange("(sc p) c -> p sc c", p=128),
                                  "ldxr%d" % ((c0 // gsz) % 8), writes=[b_xr[c] for c in range(c0, c0 + gsz)])
                    for kt in range(NH2):
                        tcs, tns, b_tc, b_tn, ti = tabr.next()
                        P.dma(tcs[:, 0:NSC * 128], tcd[kt], "ldtc%d" % ti, writes=[b_tc])
                        P.dma(tns[:, 0:NSC * 128], tsd[kt], "ldtn%d" % ti, writes=[b_tn])
                        (accE, b_E), (accO, b_O) = PAIR.next()
                        for r, (acc, b_acc) in enumerate(((accE, b_E), (accO, b_O))):
                            for mc in range(NH2):
                                sc = r * NH2 + mc
                                P.op("pe", lambda e, acc=acc, tcs=tcs, sc=sc, mc=mc: e.matmul(
                                    acc[:], lhsT=tcs[:, sc * 128:(sc + 1) * 128], rhs=xr[:, sc, 0:512],
                                    start=(mc == 0), stop=False),
                                     reads=[b_tc, b_xr[sc]], writes=[b_acc])
                                P.op("pe", lambda e, acc=acc, tns=tns, sc=sc, mc=mc, NH2=NH2: e.matmul(
                                    acc[:], lhsT=tns[:, sc * 128:(sc + 1) * 128], rhs=xr[:, sc, 512:1024],
                                    start=False, stop=(mc == NH2 - 1)),
                                     reads=[b_tn, b_xr[sc]], writes=[b_acc])
                        esb, b_es = esr.next()
                        P.op("act", lambda e, esb=esb, accE=accE, scale=scale: e.activation(
                            out=esb[:], in_=accE[:], func=AF.Copy, scale=scale),
                             reads=[b_E], writes=[b_es])
                        for sgn, rbase in ((1.0, tb + kt * 128), (-1.0, tb + S // 2 + kt * 128)):
                            of, b_of, oi = ofr.next()
                            P.op("dve", lambda e, of=of, accO=accO, esb=esb, sc_=sgn * scale: e.scalar_tensor_tensor(
                                of[:], accO[:], sc_, esb[:], ALU.mult, ALU.add),
                                 reads=[b_O, b_es], writes=[b_of])
                            P.dma(mixed[rbase:rbase + 128, hf * 512:(hf + 1) * 512], of[:], "stof%d" % oi,
                                  reads=[b_of], q="pool")
            P.end_phase()

            if upto == 2:
                return nc, P

        with ExitStack() as es:
            def sb(name, shape, dt):
                return es.enter_context(nc.sbuf_tensor(name, list(shape), dt))

            pbanks = [es.enter_context(nc.psum_tensor("p3bank%d" % i, [128, 512], F32)) for i in range(8)]
            pb = [Buf("p3bank%d" % i) for i in range(8)]
            TR = Ring([(pbanks[i][:].bitcast(BF16), pb[i]) for i in (0, 1)])
            G = Ring([(pbanks[i], pb[i]) for i in (2, 3)])
            ACCS = [(pbanks[i], pb[i]) for i in (4, 5, 6, 7)]

            wslots = [sb("w3s%d" % i, [128, 16, 512], BF16) for i in range(3)]
            wbufs = [Buf("w3s%d" % i) for i in range(3)]
            xsr = Ring([(sb("x3_%d" % i, [128, D], F32), Buf("x3_%d" % i), i) for i in range(8)])
            mxr = Ring([(sb("mx%d" % i, [128, D], BF16), Buf("mx%d" % i), i) for i in range(4)])
            junk = sb("junk3", [128, D], BF16)
            b_junk = Buf("junk3")
            aT = sb("aT", [128, 16, 512], BF16)
            b_aT = Buf("aT")
            uT = sb("uT", [128, 16, 512], BF16)
            b_uT = [Buf("uT%d" % i) for i in range(16)]
            rlr = Ring([(sb("rl%d" % i, [128, 512], F32), Buf("rl%d" % i)) for i in range(2)])
            ssr = Ring([(sb("ss3_%d" % i, [128, 2], F32), Buf("ss3_%d" % i)) for i in range(8)])
            gfin = sb("gfin_sb", [128, D], F32)
            gmlp = sb("gmlp_sb", [128, D], F32)
            b_g3 = Buf("g3")
            P.dma(gfin[:], g_fin.partition_broadcast(128), "c5", writes=[b_g3])
            P.dma(gmlp[:], g_mlp.partition_broadcast(128), "c5", writes=[b_g3])

            def rstd_of(src, b_src, width):
                ss, b_ss = ssr.next()
                P.op("act", lambda e: e.activation(out=junk[:, 0:width], in_=src, func=AF.Square,
                                                   accum_out=ss[:, 0:1]),
                     reads=[b_src], writes=[b_junk, b_ss])
                P.op("act", lambda e: e.activation(out=ss[:, 1:2], in_=ss[:, 0:1], func=AF.Ln, scale=1.0 / width,
                                                   bias=epsb[:, 0:1]),
                     reads=[b_ss, b_const], writes=[b_ss])
                P.op("act", lambda e: e.activation(out=ss[:, 1:2], in_=ss[:, 1:2], func=AF.Exp, scale=-0.5),
                     reads=[b_ss], writes=[b_ss])
                return ss, b_ss

            NT3 = NTOK // 512
            seq3 = [(PW_OUT + c, 512) for c in range(4)]
            for T in range(NT3):
                for q in range(4):
                    seq3 += [(PW_UP + q * 4 + i, 512) for i in range(4)]
                    if q == 3 and T + 1 < NT3:
                        seq3 += [(PW_OUT + c, 512) for c in range(4)]
                    seq3 += [(PW_DOWN + c * 4 + q, 512) for c in range(4)]
            ws = WStream(wslots, wbufs, seq3, "w3_")
            evq = [0]

            def transposes(hb, b_hb, j):
                for half in range(2):
                    trv, b_tr = TR.next()
                    for kk in range(8):
                        kc = half * 8 + kk
                        P.op("pe", lambda e, trv=trv, kk=kk, kc=kc, hb=hb: e.transpose(
                            trv[:, kk * 128:(kk + 1) * 128], hb[:, kc * 128:(kc + 1) * 128], ident[:]),
                             reads=[b_hb, b_const], writes=[b_tr])
                    evq[0] += 1
                    dst = aT[:, half * 8:(half + 1) * 8, j * 128:(j + 1) * 128]
                    src = trv[:, 0:1024].rearrange("p (a b) -> p a b", a=8)
                    if evq[0] % 2:
                        P.op("dve", lambda e, dst=dst, src=src: e.tensor_copy(dst, src), reads=[b_tr], writes=[b_aT])
                    else:
                        P.op("act", lambda e, dst=dst, src=src: e.activation(out=dst, in_=src, func=AF.Copy),
                             reads=[b_tr], writes=[b_aT])

            def stage_A_loads(T):
                tok0 = T * 512
                xt, mt = [], []
                for j in range(4):
                    mx, b_mx, mi = mxr.next()
                    r0 = tok0 + j * 128
                    P.dma(mx[:], mixed[r0:r0 + 128, :], "ldmx%d" % mi, writes=[b_mx])
                    xs, b_xs, xi = xsr.next()
                    P.dma(xs[:], x[r0:r0 + 128, :], "ldx3_%d" % xi, writes=[b_xs], q=("pool" if T >= 2 else "sp"))
                    xt.append((xs, b_xs, xi))
                    mt.append((mx, b_mx))
                return xt, mt

            def stage_A_compute(mt):
                for j in range(4):
                    mx, b_mx = mt[j]
                    ss, b_ss = rstd_of(mx[:, 0:1024], b_mx, 1024)
                    P.op("act", lambda e, mx=mx, ss=ss: e.activation(out=mx[:, 0:1024], in_=mx[:, 0:1024],
                                                                     func=AF.Copy, scale=ss[:, 1:2]),
                         reads=[b_mx, b_ss], writes=[b_mx])

            def stage_A(T):
                xt, mt = stage_A_loads(T)
                stage_A_compute(mt)
                return xt, mt

            def stage_BC(xt, mt):
                for j in range(4):
                    transposes(mt[j][0], mt[j][1], j)
                for c in range(4):
                    wsl, b_w = ws.get()
                    for j in range(4):
                        xs, b_xs, xi = xt[j]
                        bank, bbank = G.next()
                        for kc in range(16):
                            P.op("pe", lambda e, bank=bank, wsl=wsl, kc=kc, j=j: e.matmul(
                                bank[:], lhsT=aT[:, kc, j * 128:(j + 1) * 128], rhs=wsl[:, kc, :],
                                start=(kc == 0), stop=(kc == 15)),
                                 reads=[b_w, b_aT], writes=[bbank])
                        P.op("dve", lambda e, bank=bank, xs=xs, c=c: e.tensor_tensor(
                            xs[:, c * 512:(c + 1) * 512], bank[:], xs[:, c * 512:(c + 1) * 512], ALU.add),
                             reads=[bbank, b_xs], writes=[b_xs])
                for j in range(4):
                    xs, b_xs, xi = xt[j]
                    mx, b_mx = mt[j]
                    ss, b_ss = rstd_of(xs[:], b_xs, D)
                    P.op("dve", lambda e, mx=mx, xs=xs, ss=ss: e.scalar_tensor_tensor(
                        mx[:], xs[:], ss[:, 1:2], gmlp[:], ALU.mult, ALU.mult),
                         reads=[b_xs, b_ss, b_g3], writes=[b_mx])

            def stage_E(mt):
                for j in range(4):
                    transposes(mt[j][0], mt[j][1], j)

            def stage_up(q):
                for i in range(4):
                    wsl, b_w = ws.get()
                    for fi in range(4):
                        f = i * 4 + fi
                        bank, bbank = G.next()
                        for kc in range(16):
                            P.op("pe", lambda e, bank=bank, wsl=wsl, kc=kc, fi=fi: e.matmul(
                                bank[:], lhsT=wsl[:, kc, fi * 128:(fi + 1) * 128], rhs=aT[:, kc, :],
                                start=(kc == 0), stop=(kc == 15)),
                                 reads=[b_w, b_aT], writes=[bbank])
                        rl, b_rl = rlr.next()
                        P.op("act", lambda e, rl=rl, bank=bank: e.activation(out=rl[:], in_=bank[:], func=AF.Relu),
                             reads=[bbank], writes=[b_rl])
                        P.op("pool", lambda e, rl=rl, f=f: e.tensor_tensor(uT[:, f, :], rl[:], rl[:], ALU.mult),
                             reads=[b_rl], writes=[b_uT[f]])

            def stage_down(xt):
                for c in range(4):
                    wsl, b_w = ws.get()
                    for jp in range(2):
                        for fl in range(16):
                            for j in (2 * jp, 2 * jp + 1):
                                acc, b_acc = ACCS[j]
                                P.op("pe", lambda e, acc=acc, fl=fl, j=j, wsl=wsl: e.matmul(
                                    acc[:], lhsT=uT[:, fl, j * 128:(j + 1) * 128], rhs=wsl[:, fl, :],
                                    start=(fl == 0), stop=(fl == 15)),
                                     reads=[b_w, b_uT[fl]], writes=[b_acc])
                        for j in (2 * jp, 2 * jp + 1):
                            xs, b_xs, xi = xt[j]
                            acc, b_acc = ACCS[j]
                            P.op("dve", lambda e, acc=acc, xs=xs, c=c: e.tensor_tensor(
                                xs[:, c * 512:(c + 1) * 512], acc[:], xs[:, c * 512:(c + 1) * 512], ALU.add),
                                 reads=[b_acc, b_xs], writes=[b_xs])

            def stage_H(T, xt):
                tok0 = T * 512
                for j in range(4):
                    xs, b_xs, xi = xt[j]
                    ss, b_ss = rstd_of(xs[:], b_xs, D)
                    P.op("dve", lambda e, xs=xs, ss=ss: e.scalar_tensor_tensor(
                        xs[:], xs[:], ss[:, 1:2], gfin[:], ALU.mult, ALU.mult),
                         reads=[b_xs, b_ss, b_g3], writes=[b_xs])
                    r0 = tok0 + j * 128
                    P.dma(y[r0:r0 + 128, :], xs[:], "sty%d" % xi, reads=[b_xs], q="pool")

            xt, mt = stage_A(0)
            stage_BC(xt, mt)
            prev_H = None
            for T in range(NT3):
                stage_E(mt)
                nxt = None
                for q in range(4):
                    stage_up(q)
                    if q == 0:
                        if prev_H is not None:
                            stage_H(*prev_H)
                        if T + 1 < NT3:
                            nxt = stage_A_loads(T + 1)
                    if q == 3 and nxt is not None:
                        stage_BC(nxt[0], nxt[1])
                    stage_down(xt)
                    if q == 0 and nxt is not None:
                        stage_A_compute(nxt[1])
                prev_H = (T, xt)
                if nxt is not None:
                    xt, mt = nxt
            stage_H(*prev_H)
            P.end_phase()

    return nc, P


_NC_CACHE = {}


def run_cores(xs_per_core, S_list, weights, debug=False, upto=3):
    key = (tuple(S_list), debug, upto)
    if key not in _NC_CACHE:
        _NC_CACHE[key] = build(S_list, debug, upto)[0]
    nc = _NC_CACHE[key]
    consts = _consts(set(S_list), max(S_list))
    in_maps = []
    for xc in xs_per_core:
        m = {"x": xc}
        m.update(weights)
        m.update(consts)
        in_maps.append(m)
    res = run_bass_kernel_spmd(nc, in_maps, core_ids=list(range(len(xs_per_core))))
    return res


def kernel(x_prompt, x_sample, ln_mix_g, w_in, w_fourier, attn_sink, out_norm_fourier_g,
           out_norm_attn_g, w_out, ln_mlp_g, w_up, w_down, ln_final_g):
    f = lambda a: np.ascontiguousarray(np.asarray(a, dtype=np.float32))
    x_prompt = f(x_prompt)
    x_sample = f(x_sample)
    BP, SP, _ = x_prompt.shape
    BS, SS, _ = x_sample.shape
    assert BP == N_CORES and BS == 2 * N_CORES
    S_list = [SP, SS, SS]
    weights = {
        "w_in": f(w_in)[0], "w_fourier": f(w_fourier)[0], "attn_sink": f(attn_sink)[0],
        "ln_mix_g": f(ln_mix_g)[0], "out_norm_fourier_g": f(out_norm_fourier_g)[0],
        "out_norm_attn_g": f(out_norm_attn_g)[0], "w_out": f(w_out)[0], "ln_mlp_g": f(ln_mlp_g)[0],
        "w_up": f(w_up)[0], "w_down": f(w_down)[0], "ln_final_g": f(ln_final_g),
    }
    xs = []
    for i in range(N_CORES):
        xs.append(np.concatenate([x_prompt[i], x_sample[2 * i], x_sample[2 * i + 1]], axis=0))
    res = run_cores(xs, S_list, weights)
    yp = np.empty((BP, SP, D), np.float32)
    ysm = np.empty((BS, SS, D), np.float32)
    for i in range(N_CORES):
        yc = res.results[i]["y"]
        yp[i] = yc[0:SP]
        ysm[2 * i] = yc[SP:SP + SS]
        ysm[2 * i + 1] = yc[SP + SS:SP + 2 * SS]
    return (yp, ysm)
```
